# Optimizing a Trainium2 kernel written in Bass

```python
import math
import jax, jax.numpy as jnp
from jax import lax
import numpy as np

D_MODEL = 1024
BATCH = 16
SEQ = 256
DEPTH = 1
DEC_BATCH = 8
DEC_SEQ = 1024
PAST_LEN = 256

GRID_W = 64
CHUNK = 64
EPS = 1e-6
N_MOD = 9
D_FF = 2816
GLA_HEADS = 4
GLA_DK = 128
GLA_DV = 256
GLA_LOWRANK = 16
GLA_TAU = 16.0
GLA_QK = GLA_HEADS * GLA_DK
GLA_V = GLA_HEADS * GLA_DV
SSD_HEADS = 16
SSD_HEAD_DIM = 64
SSD_GROUPS = 2
SSD_STATE = 128
SSD_INNER = SSD_HEADS * SSD_HEAD_DIM
SSD_BC = SSD_GROUPS * SSD_STATE
SSD_CONV_DIM = SSD_INNER + 2 * SSD_BC
CONV_K = 3
D_MIX = GLA_V + SSD_INNER
IN_SIZES = (GLA_QK, GLA_QK, GLA_V, GLA_V, GLA_LOWRANK, GLA_LOWRANK,
            SSD_INNER, SSD_CONV_DIM, SSD_HEADS, SSD_HEADS)
N_IN = sum(IN_SIZES)

kernel_name = 'hymba_gla_ssd_macaron_dit_step'


def _rmsnorm(x, w):
    x32 = x.astype(jnp.float32)
    y = x32 * lax.rsqrt(jnp.mean(x32 * x32, axis=-1, keepdims=True) + EPS)
    return (y * w.astype(jnp.float32)).astype(x.dtype)


def _swiglu(h, w_in, w_out):
    g, u = jnp.split(h @ w_in, 2, axis=-1)
    return (jax.nn.silu(g) * u) @ w_out


def _gla_scan(q, k, v, log_a, h0):
    b, l, h, dk = q.shape
    dv = v.shape[-1]
    nc = l // CHUNK
    f32 = jnp.float32
    q = q.astype(f32).reshape(b, nc, CHUNK, h, dk)
    k = k.astype(f32).reshape(b, nc, CHUNK, h, dk)
    v = v.astype(f32).reshape(b, nc, CHUNK, h, dv)
    g = jnp.cumsum(log_a.astype(f32).reshape(b, nc, CHUNK, h, dk), axis=2)
    g_ref = g[:, :, CHUNK // 2:CHUNK // 2 + 1]
    q_i = q * jnp.exp(g - g_ref)
    k_i = k * jnp.exp(g_ref - g)
    causal = jnp.tril(jnp.ones((CHUNK, CHUNK), bool))
    scores = jnp.einsum('bcthd,bcshd->bchts', q_i, k_i)
    scores = jnp.where(causal, scores, 0.0)
    o_intra = jnp.einsum('bchts,bcshv->bcthv', scores, v)
    g_last = g[:, :, -1]
    k_state = k * jnp.exp(g_last[:, :, None] - g)
    chunk_kv = jnp.einsum('bcshd,bcshv->bchdv', k_state, v)
    chunk_decay = jnp.exp(g_last)

    def step(state, inp):
        kv_c, dec_c = inp
        return state * dec_c[..., None] + kv_c, state

    final, prev = lax.scan(step, h0.astype(f32),
                           (jnp.moveaxis(chunk_kv, 1, 0), jnp.moveaxis(chunk_decay, 1, 0)))
    prev = jnp.moveaxis(prev, 0, 1)
    o_inter = jnp.einsum('bcthd,bchdv->bcthv', q * jnp.exp(g), prev)
    return (o_intra + o_inter).reshape(b, l, h, dv), final


def _ssd_scan(x, dt, a, bm, cm, h0):
    b, l, h, p = x.shape
    n = bm.shape[-1]
    nc = l // CHUNK
    f32 = jnp.float32
    x = x.astype(f32).reshape(b, nc, CHUNK, h, p)
    dt = dt.astype(f32).reshape(b, nc, CHUNK, h)
    bm = bm.astype(f32).reshape(b, nc, CHUNK, h, n)
    cm = cm.astype(f32).reshape(b, nc, CHUNK, h, n)
    cum = jnp.cumsum(dt * a, axis=2)
    causal = jnp.tril(jnp.ones((CHUNK, CHUNK), bool))[None, None, :, :, None]
    seg = cum[:, :, :, None, :] - cum[:, :, None, :, :]
    decay_ts = jnp.exp(jnp.where(causal, seg, -jnp.inf))
    scores = jnp.einsum('bcthn,bcshn->bctsh', cm, bm) * decay_ts * dt[:, :, None, :, :]
    y_intra = jnp.einsum('bctsh,bcshp->bcthp', scores, x)
    w_end = jnp.exp(cum[:, :, -1:, :] - cum) * dt
    chunk_state = jnp.einsum('bcsh,bcshn,bcshp->bchpn', w_end, bm, x)
    chunk_decay = jnp.exp(cum[:, :, -1, :])

    def step(state, inp):
        st_c, dec_c = inp
        return state * dec_c[:, :, None, None] + st_c, state

    final, prev = lax.scan(step, h0.astype(f32),
                           (jnp.moveaxis(chunk_state, 1, 0), jnp.moveaxis(chunk_decay, 1, 0)))
    prev = jnp.moveaxis(prev, 0, 1)
    y_inter = jnp.einsum('bcthn,bchpn->bcthp', cm, prev) * jnp.exp(cum)[..., None]
    return (y_intra + y_inter).reshape(b, l, h, p), final


def _dwconv_grid(u, w, bias, rows):
    b, l, ch = u.shape
    grid = u.reshape(b, rows, l // rows, ch)
    out = lax.conv_general_dilated(grid, w[:, :, None, :].astype(u.dtype), (1, 1), 'SAME',
                                   dimension_numbers=('NHWC', 'HWIO', 'NHWC'),
                                   feature_group_count=ch)
    return out.reshape(b, l, ch) + bias


def _mixer(h, lw, gla_h0, ssd_h0, rows):
    b, l, _ = h.shape
    f32 = jnp.float32
    flip = lambda t: jnp.flip(t, 1)
    split_idx = np.cumsum(IN_SIZES)[:-1].tolist()
    q, k, v, r, af, ab, z, xbc, dtf, dtb = jnp.split(h @ lw['w_in'], split_idx, axis=-1)
    q = q.reshape(b, l, GLA_HEADS, GLA_DK) * (GLA_DK ** -0.5)
    k = k.reshape(b, l, GLA_HEADS, GLA_DK)
    v = v.reshape(b, l, GLA_HEADS, GLA_DV)

    def log_gate(lr, d):
        pre = (lr @ lw['gla_w_a2'][d] + lw['gla_b_a'][d]).astype(f32)
        return (jax.nn.log_sigmoid(pre) / GLA_TAU).reshape(b, l, GLA_HEADS, GLA_DK)

    o_f, sg_f = _gla_scan(q, k, v, log_gate(af, 0), gla_h0[:, 0])
    o_b, sg_b = _gla_scan(flip(q), flip(k), flip(v), flip(log_gate(ab, 1)), gla_h0[:, 1])
    o = (o_f + flip(o_b)).astype(h.dtype)
    o = _rmsnorm(o, lw['gla_norm_w']) * jax.nn.silu(r.reshape(b, l, GLA_HEADS, GLA_DV))
    xbc = jax.nn.silu(_dwconv_grid(xbc, lw['conv_w'], lw['conv_b'], rows))
    xs, bm, cm = jnp.split(xbc, [SSD_INNER, SSD_INNER + SSD_BC], axis=-1)
    xs = xs.reshape(b, l, SSD_HEADS, SSD_HEAD_DIM)
    rep = SSD_HEADS // SSD_GROUPS
    bm = jnp.repeat(bm.reshape(b, l, SSD_GROUPS, SSD_STATE), rep, axis=2)
    cm = jnp.repeat(cm.reshape(b, l, SSD_GROUPS, SSD_STATE), rep, axis=2)

    def dt_a(raw, d):
        dt = jax.nn.softplus(raw.astype(f32) + lw['dt_bias'][d].astype(f32))
        return dt, -jnp.exp(lw['a_log'][d].astype(f32))

    dt_f, a_f = dt_a(dtf, 0)
    dt_b, a_b = dt_a(dtb, 1)
    y_f, ss_f = _ssd_scan(xs, dt_f, a_f, bm, cm, ssd_h0[:, 0])
    y_b, ss_b = _ssd_scan(flip(xs), flip(dt_b), a_b, flip(bm), flip(cm), ssd_h0[:, 1])
    y = y_f + flip(y_b) + xs.astype(f32) * lw['d_skip'].astype(f32)[:, None]
    y = _rmsnorm(y.reshape(b, l, SSD_INNER).astype(h.dtype) * jax.nn.silu(z), lw['ssd_norm_w'])
    out = jnp.concatenate([o.reshape(b, l, GLA_V), y], axis=-1) @ lw['w_out']
    return out, jnp.stack([sg_f, sg_b], axis=1), jnp.stack([ss_f, ss_b], axis=1)


def _layer(x, mod, lw, gla_h0, ssd_h0, rows):
    sh1, sc1, g1, sh2, sc2, g2, sh3, sc3, g3 = jnp.split(mod, N_MOD, axis=-1)
    h = _rmsnorm(x, lw['norm_ffn1']) * (1 + sc1) + sh1
    x = x + 0.5 * g1 * _swiglu(h, lw['ffn1_w_in'], lw['ffn1_w_out'])
    h = _rmsnorm(x, lw['norm_mix']) * (1 + sc2) + sh2
    m, sg, ss = _mixer(h, lw, gla_h0, ssd_h0, rows)
    x = x + g2 * m
    h = _rmsnorm(x, lw['norm_ffn2']) * (1 + sc3) + sh3
    x = x + 0.5 * g3 * _swiglu(h, lw['ffn2_w_in'], lw['ffn2_w_out'])
    return x, sg, ss


def setup_inputs(seed: int = 0) -> dict:
    key = jax.random.key(seed)
    ks = jax.random.split(key, 32)
    f32 = jnp.float32
    nrm = lambda k, shape, scale: jax.random.normal(k, shape, f32) * scale
    dt0 = jnp.exp(jax.random.uniform(ks[20], (DEPTH, 2, SSD_HEADS), f32,
                                     math.log(1e-3), math.log(1e-1)))
    return {
        'x_prompt': nrm(ks[0], (BATCH, SEQ, D_MODEL), 1.0),
        'x_sample': nrm(ks[1], (DEC_BATCH, DEC_SEQ, D_MODEL), 1.0),
        'state_gla': nrm(ks[2], (DEC_BATCH, DEPTH, 2, GLA_HEADS, GLA_DK, GLA_DV), 1.0),
        'state_ssd': nrm(ks[3], (DEC_BATCH, DEPTH, 2, SSD_HEADS, SSD_HEAD_DIM, SSD_STATE), 0.1),
        'c': nrm(ks[4], (DEC_BATCH, D_MODEL), 1.0),
        'c_ctx': nrm(ks[5], (D_MODEL,), 1.0),
        'norm_ffn1': 1.0 + nrm(ks[6], (DEPTH, D_MODEL), 0.02),
        'norm_mix': 1.0 + nrm(ks[7], (DEPTH, D_MODEL), 0.02),
        'norm_ffn2': 1.0 + nrm(ks[8], (DEPTH, D_MODEL), 0.02),
        'w_mod': nrm(ks[9], (DEPTH, D_MODEL, N_MOD * D_MODEL), 0.5 * D_MODEL ** -0.5),
        'b_mod': nrm(ks[10], (DEPTH, N_MOD * D_MODEL), 0.02),
        'ffn1_w_in': nrm(ks[11], (DEPTH, D_MODEL, 2 * D_FF), D_MODEL ** -0.5),
        'ffn1_w_out': nrm(ks[12], (DEPTH, D_FF, D_MODEL), D_FF ** -0.5),
        'ffn2_w_in': nrm(ks[13], (DEPTH, D_MODEL, 2 * D_FF), D_MODEL ** -0.5),
        'ffn2_w_out': nrm(ks[14], (DEPTH, D_FF, D_MODEL), D_FF ** -0.5),
        'w_in': nrm(ks[15], (DEPTH, D_MODEL, N_IN), D_MODEL ** -0.5),
        'gla_w_a2': nrm(ks[16], (DEPTH, 2, GLA_LOWRANK, GLA_QK), GLA_LOWRANK ** -0.5),
        'gla_b_a': nrm(ks[17], (DEPTH, 2, GLA_QK), 0.1),
        'gla_norm_w': 1.0 + nrm(ks[18], (DEPTH, GLA_DV), 0.02),
        'conv_w': nrm(ks[19], (DEPTH, CONV_K, CONV_K, SSD_CONV_DIM), 1.0 / CONV_K),
        'conv_b': nrm(ks[21], (DEPTH, SSD_CONV_DIM), 0.02),
        'dt_bias': dt0 + jnp.log(-jnp.expm1(-dt0)),
        'a_log': jnp.log(jax.random.uniform(ks[22], (DEPTH, 2, SSD_HEADS), f32, 1.0, 16.0)),
        'd_skip': 1.0 + nrm(ks[23], (DEPTH, SSD_HEADS), 0.1),
        'ssd_norm_w': 1.0 + nrm(ks[24], (DEPTH, SSD_INNER), 0.02),
        'w_out': nrm(ks[25], (DEPTH, D_MIX, D_MODEL), D_MIX ** -0.5),
        'final_norm': 1.0 + nrm(ks[26], (D_MODEL,), 0.02),
    }


def reference(x_prompt, x_sample, state_gla, state_ssd, c, c_ctx, norm_ffn1, norm_mix,
              norm_ffn2, w_mod, b_mod, ffn1_w_in, ffn1_w_out, ffn2_w_in, ffn2_w_out, w_in,
              gla_w_a2, gla_b_a, gla_norm_w, conv_w, conv_b, dt_bias, a_log, d_skip,
              ssd_norm_w, w_out, final_norm):
    xp, xs = x_prompt, x_sample
    rows = xs.shape[1] // GRID_W
    gla_states, ssd_states = [], []
    for i in range(DEPTH):
        lw = {'norm_ffn1': norm_ffn1[i], 'norm_mix': norm_mix[i], 'norm_ffn2': norm_ffn2[i],
              'ffn1_w_in': ffn1_w_in[i], 'ffn1_w_out': ffn1_w_out[i],
              'ffn2_w_in': ffn2_w_in[i], 'ffn2_w_out': ffn2_w_out[i],
              'w_in': w_in[i], 'gla_w_a2': gla_w_a2[i], 'gla_b_a': gla_b_a[i],
              'gla_norm_w': gla_norm_w[i], 'conv_w': conv_w[i], 'conv_b': conv_b[i],
              'dt_bias': dt_bias[i], 'a_log': a_log[i], 'd_skip': d_skip[i],
              'ssd_norm_w': ssd_norm_w[i], 'w_out': w_out[i]}
        mod_ctx = (jax.nn.silu(c_ctx) @ w_mod[i] + b_mod[i])[None, None, :]
        mod_lat = (jax.nn.silu(c) @ w_mod[i] + b_mod[i])[:, None, :]
        nb = xp.shape[0]
        gla0 = jnp.zeros((nb, 2, GLA_HEADS, GLA_DK, GLA_DV), jnp.float32)
        ssd0 = jnp.zeros((nb, 2, SSD_HEADS, SSD_HEAD_DIM, SSD_STATE), jnp.float32)
        xp, sg, ss = _layer(xp, mod_ctx, lw, gla0, ssd0, 1)
        gla_states.append(sg)
        ssd_states.append(ss)
        xs, _, _ = _layer(xs, mod_lat, lw, state_gla[:, i], state_ssd[:, i], rows)
    y_prompt = _rmsnorm(xp, final_norm)
    y_sample = _rmsnorm(xs, final_norm)
    new_state_gla = jnp.stack(gla_states, axis=1)
    new_state_ssd = jnp.stack(ssd_states, axis=1)
    return (y_prompt, y_sample, new_state_gla, new_state_ssd)
```

```python
import contextlib
import numpy as np
import concourse.bass as bass
import concourse.mybir as mybir
from concourse.bass_utils import run_bass_kernel_spmd

F32 = mybir.dt.float32
BF16 = mybir.dt.bfloat16
AF = mybir.ActivationFunctionType
ALU = mybir.AluOpType

D = 1024
TOK = 1536
NCH = 12
DFF = 2816
NIN = 5696
EPS = 1e-6
SB_BYTES = 206 * 1024
PAGE = 512
PS_PAGE = 2048
SAME_SYNC = True

ENGS = ["pe", "act", "dve", "pool", "sp"]


class Chan:
    def __init__(self, sem, serial=True):
        self.sem = sem
        self.n = 0
        self.serial = serial


class Op:
    __slots__ = ("fn", "deps", "chan", "signal", "extra")

    def __init__(self, fn, deps, chan):
        self.fn = fn
        self.deps = deps
        self.chan = chan
        self.signal = False
        self.extra = None


def _esize(dt):
    return 2 if dt == BF16 else 4


class Prog:
    def __init__(self, nc, sbname, psname):
        self.nc = nc
        self.sbname = sbname
        self.psname = psname
        self.ops = {e: [] for e in ENGS}
        self.pages = {}
        self.seen = {e: {} for e in ENGS}
        self.chans = []

    def pages_of(self, ap):
        name = ap.tensor.name
        if name == self.sbname:
            row, pg, tag = SB_BYTES, PAGE, "s"
        elif name == self.psname:
            row, pg, tag = 16384, PS_PAGE, "p"
        else:
            return []
        es = _esize(ap.dtype)
        lo = (int(ap.offset) * es) % row
        ext = 1
        for (s, c) in ap.ap[1:]:
            ext += (c - 1) * abs(s)
        hi = lo + ext * es
        assert hi <= row, (lo, hi, ap)
        return [(tag, i) for i in range(lo // pg, (hi - 1) // pg + 1)]

    def record(self, eng, fn, reads, writes, chan=None):
        if chan is not None:
            chan.n += 1
            me = ("d", chan, chan.n)
        else:
            me = ("e", eng, len(self.ops[eng]))
        deps = {}

        def add(dep):
            k = (dep[0], dep[1])
            if deps.get(k, -1) < dep[2]:
                deps[k] = dep[2]

        rp = []
        for ap in reads:
            rp += self.pages_of(ap)
        wp = []
        for ap in writes:
            wp += self.pages_of(ap)
        for pg in rp:
            st = self.pages.get(pg)
            if st and st[0]:
                add(st[0])
            if st and pg[0] == "p":
                for rk, r in st[1].items():
                    if rk != (me[0], me[1]):
                        add(r)
        for pg in wp:
            st = self.pages.get(pg)
            if st:
                if st[0]:
                    add(st[0])
                for r in st[1].values():
                    add(r)
        if chan is not None and chan.serial and chan.n > 1:
            add(("d", chan, chan.n - 1))
        final = []
        for (kind, key), val in deps.items():
            if kind == "e" and key == eng:
                if eng in ("pe", "sp"):
                    continue
                if not SAME_SYNC and chan is None:
                    continue
            if kind == "d" and chan is not None and key is chan and not chan.serial:
                continue
            if self.seen[eng].get((kind, key), -1) >= val:
                continue
            self.seen[eng][(kind, key)] = val
            final.append((kind, key, val))
            if kind == "e":
                self.ops[key][val].signal = True
        for pg in rp:
            st = self.pages.setdefault(pg, [None, {}])
            st[1][(me[0], me[1])] = me
        for pg in wp:
            st = self.pages.setdefault(pg, [None, {}])
            st[0] = me
            st[1] = {}
        op = Op(fn, final, chan)
        self.ops[eng].append(op)
        return op

    def mm(self, out, lhsT, rhs, start=True, stop=True):
        self.record("pe", lambda e: e.matmul(out, lhsT, rhs, start=start, stop=stop),
                    [lhsT, rhs] + ([] if start else []), [out])

    def transpose(self, out, in_, ident):
        self.record("pe", lambda e: e.transpose(out, in_, ident), [in_, ident], [out])

    def act(self, out, in_, func, bias=None, scale=None, accum_out=None):
        reads = [in_]
        kw = {}
        writes = [out]
        if accum_out is not None:
            kw["accum_out"] = accum_out
            writes.append(accum_out)
        if bias is not None:
            kw["bias"] = bias
            if not isinstance(bias, (int, float)):
                reads.append(bias)
        if scale is not None:
            kw["scale"] = scale
            if not isinstance(scale, (int, float)):
                reads.append(scale)
        self.record("act", lambda e: e.activation(out, in_, func, **kw), reads, writes)

    def tt(self, out, in0, in1, op, eng="dve"):
        self.record(eng, lambda e: e.tensor_tensor(out, in0, in1, op), [in0, in1], [out])

    def ts(self, out, in0, s1, s2, op0, op1=None, eng="dve"):
        reads = [in0]
        if not isinstance(s1, (int, float)):
            reads.append(s1)
        if s2 is not None and not isinstance(s2, (int, float)):
            reads.append(s2)
        if op1 is None:
            self.record(eng, lambda e: e.tensor_scalar(out, in0, s1, None, op0), reads, [out])
        else:
            self.record(eng, lambda e: e.tensor_scalar(out, in0, s1, s2, op0, op1), reads, [out])

    def stt(self, out, in0, scalar, in1, op0, op1, eng="dve"):
        reads = [in0, in1]
        if not isinstance(scalar, (int, float)):
            reads.append(scalar)
        self.record(eng, lambda e: e.scalar_tensor_tensor(out, in0, scalar, in1, op0, op1),
                    reads, [out])

    def copy(self, out, in_, eng="dve"):
        if eng == "act":
            self.record("act", lambda e: e.copy(out, in_), [in_], [out])
        else:
            self.record(eng, lambda e: e.tensor_copy(out, in_), [in_], [out])

    def memset(self, out, val, eng="dve"):
        self.record(eng, lambda e: e.memset(out, val), [], [out])

    def dma(self, q, out, in_, chan):
        self.record(q, lambda e: e.dma_start(out=out, in_=in_), [in_], [out], chan=chan)

    def emit(self, esem):
        nc = self.nc
        sig = {}
        for e in ENGS:
            cnt = 0
            arr = []
            for op in self.ops[e]:
                if op.signal:
                    cnt += 1
                arr.append(cnt)
            sig[e] = arr
        chans = self.chans

        def run(name, eng):
            for op in self.ops[name]:
                for (kind, key, val) in op.deps:
                    if kind == "e":
                        eng.wait_ge(esem[key], sig[key][val])
                    else:
                        eng.wait_ge(key.sem, 16 * (val if key.serial else key.n))
                ins = op.fn(eng)
                if op.chan is not None:
                    ins.then_inc(op.chan.sem, 16)
                elif op.signal:
                    ins.then_inc(esem[name], 1)
            if name == "sp":
                for ch in chans:
                    if ch.n:
                        eng.wait_ge(ch.sem, 16 * ch.n)

        with nc.Block() as block:
            @block.tensor
            def _(e):
                run("pe", e)

            @block.scalar
            def _(e):
                run("act", e)

            @block.vector
            def _(e):
                run("dve", e)

            @block.gpsimd
            def _(e):
                run("pool", e)

            @block.sync
            def _(e):
                run("sp", e)


class Alloc:
    def __init__(self, sb):
        self.sb = sb
        self.off = 0

    def take(self, nbytes, align=PAGE):
        self.off = (self.off + align - 1) // align * align
        o = self.off
        self.off += nbytes
        assert self.off <= SB_BYTES, self.off
        return o

    def view(self, off, nelem, dt):
        a = self.sb[:, off // 4:(off + nelem * _esize(dt)) // 4]
        if dt != F32:
            a = a.bitcast(dt)
        return a


NCONST = 8


def host_consts():
    i = np.arange(128)
    u = i[:, None]
    t = i[None, :]
    ref = 64
    ident = (u == t)
    triF = (u <= t)
    triB = (u >= t)
    d1F = (u <= t).astype(np.float32) - (u <= ref).astype(np.float32)
    d1B = (u >= t).astype(np.float32) - (u >= ref).astype(np.float32)
    d2F = (u > t)
    d2B = (u < t)
    ones = np.ones((128, 128))
    return np.concatenate([np.asarray(m, np.float32) for m in
                           (ident, d1F, triF, d1B, triB, d2F, d2B, ones)], axis=1)


import os
FLAGS = {"gla": True, "ssd": True, "sgp": True, "sgs": True, "gbwd": True, "gfwd": True, "gout": True, "gnorm": True}


def build(stage=2, debug=False):
    for k_ in FLAGS:
        if os.environ.get("MK_" + k_.upper()) is not None:
            FLAGS[k_] = os.environ["MK_" + k_.upper()] == "1"
    nc = bass.Bass("TRN2", target_bir_lowering=False)

    def din(name, shape):
        return nc.dram_tensor(name, list(shape), F32, kind="ExternalInput").ap()

    def dout(name, shape):
        return nc.dram_tensor(name, list(shape), F32, kind="ExternalOutput").ap()

    xin = din("xin", [TOK, D])
    rt = din("rt", [240, 128])
    consts_d = din("consts", [128, NCONST * 128])
    w_mod = din("w_mod", [D, 9 * D])
    f1_in = din("ffn1_w_in", [D, 2 * DFF])
    f1_out = din("ffn1_w_out", [DFF, D])
    f2_in = din("ffn2_w_in", [D, 2 * DFF])
    f2_out = din("ffn2_w_out", [DFF, D])
    w_in = din("w_in", [D, NIN])
    w_out = din("w_out", [2 * D, D])
    w_a2 = din("gla_w_a2", [2, 16, 512])
    b_a = din("gla_b_a", [2, 512])
    gnw = din("gla_norm_w", [1, 256])
    dtb = din("dt_bias", [1, 32])
    alog = din("a_log", [1, 32])
    dsk = din("d_skip", [1, 16])
    snw = din("ssd_norm_w", [1, 1024])
    sg_in = din("sg", [2, 4, 128, 256])
    ss_in = din("ss", [2, 16, 64, 128])
    y_out = dout("y", [TOK, D])
    nsg_out = dout("nsg", [2, 2, 4, 128, 256])
    nss_out = dout("nss", [2, 2, 16, 64, 128])
    dbg = {}

    with contextlib.ExitStack() as es:
        sb = es.enter_context(nc.sbuf_tensor("SB", [128, SB_BYTES // 4], F32))
        ps = es.enter_context(nc.psum_tensor("PS", [128, 4096], F32))
        P = Prog(nc, "SB", "PS")
        esem = {e: es.enter_context(nc.semaphore("s_" + e)) for e in ENGS}

        def chan(name, serial=True):
            c = Chan(es.enter_context(nc.semaphore(name)), serial)
            P.chans.append(c)
            return c

        A = Alloc(sb)
        o_xT = A.take(8 * TOK * 4)
        xT = A.view(o_xT, 8 * TOK, F32).rearrange("p (k t) -> p k t", k=8)
        o_hT = A.take(8 * TOK * 2)
        hT = A.view(o_hT, 8 * TOK, BF16).rearrange("p (k t) -> p k t", k=8)
        o_c = A.take(NCONST * 128 * 4)
        cst = A.view(o_c, NCONST * 128, F32)
        identF = cst[:, 0:128]
        onesF = cst[:, 7 * 128:8 * 128]
        o_cb = A.take(NCONST * 128 * 2)
        cstb = A.view(o_cb, NCONST * 128, BF16)
        identB, triFb, triBb, onesB = (cstb[:, i * 128:(i + 1) * 128] for i in (0, 2, 4, 7))
        o_cols = A.take(240 * 4)
        cols = A.view(o_cols, 240, F32)
        o_mod = A.take(144 * 4)
        modT = A.view(o_mod, 144, F32).rearrange("p (j v) -> p j v", v=2)
        o_scl = A.take(3 * 16 * 4)
        sclT = A.view(o_scl, 48, F32).rearrange("p (i k v) -> p i k v", i=3, k=8)
        o_gat = A.take(3 * 16 * 4)
        gatT = A.view(o_gat, 48, F32).rearrange("p (i k v) -> p i k v", i=3, k=8)
        o_sc = A.take(16 * 2)
        scT = A.view(o_sc, 16, BF16)
        SLOT = 8192
        NSLOT = 4
        slots = []
        for i in range(NSLOT):
            o = A.take(SLOT)
            slots.append((o, chan("slot%d" % i)))
        o_rstd = [A.take(2048) for _ in range(2)]
        o_ln = A.take(2048)
        o_tmp = [A.take(2048) for _ in range(3)]
        o_stage = [A.take(4096) for _ in range(2)]
        stage_ch = [chan("stage%d" % i) for i in range(2)]
        arena0 = A.take(0)
        ARENA = SB_BYTES - arena0

        def arena(off, nelem, dt):
            assert off + nelem * _esize(dt) <= ARENA, (off, nelem)
            return A.view(arena0 + off, nelem, dt)

        bank_i = [0]
        bank_mod = [8]

        def bank():
            b = bank_i[0] % bank_mod[0]
            bank_i[0] += 1
            return ps[:, b * 512:(b + 1) * 512]

        slot_i = [0]

        def next_slot():
            s = slots[slot_i[0] % NSLOT]
            slot_i[0] += 1
            return s

        rr = {"tmp": 0, "rstd": 0, "ev": 0}

        def tmpbuf(dt=F32):
            o = o_tmp[rr["tmp"] % 3]
            rr["tmp"] += 1
            return A.view(o, 512, dt)

        cg = chan("constgrp", serial=False)
        P.dma("sp", cst, consts_d, cg)
        rtA = arena(0, 128, F32)
        rtB = arena(512, 128, F32)
        P.dma("sp", rtA[0:120, :], rt[0:120, :], cg)
        P.dma("sp", rtB[0:120, :], rt[120:240, :], cg)
        P.copy(cstb, cst)
        bk = bank()
        P.transpose(bk[:, 0:120], rtA[0:120, :], identF[0:120, 0:120])
        P.transpose(bk[:, 128:248], rtB[0:120, :], identF[0:120, 0:120])
        P.copy(cols[:, 0:120], bk[:, 0:120])
        P.copy(cols[:, 120:240], bk[:, 128:248])
        colsB = cols[:, 120:240]
        P.act(scT, cols[:, 72:88], AF.Silu)
        def mod_block(hb):
            so, sch = next_slot()
            sv = A.view(so, 8 * 512, BF16).rearrange("p (k c) -> p k c", k=8)
            P.dma("pool", sv, w_mod[:, hb * 512:(hb + 1) * 512].rearrange("(k p) c -> p k c", p=128), sch)
            bk = bank()
            for mm_ in range(4):
                for k in range(8):
                    P.mm(bk[:, 2 * mm_:2 * mm_ + 2], sv[:, k, mm_ * 128:(mm_ + 1) * 128],
                         scT[:, k:16:8], start=(k == 0), stop=(k == 7))
            P.tt(modT[:, hb * 4:hb * 4 + 4, :], bk[:, 0:8].rearrange("p (j v) -> p j v", v=2),
                 cols[:, hb * 4:hb * 4 + 4].unsqueeze(2).to_broadcast([128, 4, 2]), ALU.add)

        def mod_scl(i):
            nw = cols[:, 88 + 8 * i:96 + 8 * i].unsqueeze(2).to_broadcast([128, 8, 2])
            P.stt(sclT[:, i], modT[:, (3 * i + 1) * 8:(3 * i + 2) * 8, :], 1.0, nw, ALU.add, ALU.mult)

        def mod_gate(i):
            P.ts(gatT[:, i], modT[:, (3 * i + 2) * 8:(3 * i + 3) * 8, :], 0.5 if i != 1 else 1.0, None, ALU.mult)

        def mod_derived(i):
            mod_scl(i)
            mod_gate(i)

        def mod_next():
            hb_ = mod_rest.pop(0)
            mod_block(hb_)
            if hb_ == 5:
                mod_gate(0)

        xst = [arena(2048 + c * 4096, 1024, F32) for c in range(NCH)]
        xch = [chan("xld%d" % c) for c in range(NCH)]
        for c in range(NCH):
            P.dma("sp", xst[c], xin[c * 128:(c + 1) * 128, :], xch[c])
        for c in range(NCH):
            st = xst[c]
            for half in range(2):
                bk = bank()
                for kk in range(4):
                    k = half * 4 + kk
                    P.transpose(bk[:, kk * 128:(kk + 1) * 128], st[:, k * 128:(k + 1) * 128], identF)
                P.copy(xT[:, half * 4:half * 4 + 4, c * 128:(c + 1) * 128],
                       bk.rearrange("p (k t) -> p k t", k=4), eng=("dve" if half == 0 else "act"))

        for hb in range(4):
            mod_block(hb)
        mod_scl(0)
        mod_rest = list(range(4, 18))

        def rstd_tile(t, sq_off):
            if sq_off is None:
                assert o_stage[1] == o_stage[0] + 4096
                sq = A.view(o_stage[0], 8 * 512, BF16).rearrange("p (k t) -> p k t", k=8)
            else:
                sq = arena(sq_off, 8 * 512, BF16).rearrange("p (k t) -> p k t", k=8)
            for k in range(8):
                P.act(sq[:, k], xT[:, k, t * 512:(t + 1) * 512], AF.Square)
            bk = bank()
            for k in range(8):
                P.mm(bk, onesB, sq[:, k], start=(k == 0), stop=(k == 7))
            lnv = A.view(o_ln, 512, F32)
            P.act(lnv, bk, AF.Ln, bias=EPS, scale=1.0 / D)
            rs = A.view(o_rstd[rr["rstd"] % 2], 512, F32)
            rr["rstd"] += 1
            P.act(rs, lnv, AF.Exp, scale=-0.5)
            return rs

        def norm_mod(i, sq_off):
            for t in range(3):
                v = 0 if t == 0 else 1
                rs = rstd_tile(t, sq_off)
                for k in range(8):
                    tm = tmpbuf()
                    P.tt(tm, xT[:, k, t * 512:(t + 1) * 512], rs, ALU.mult)
                    if k % 2 == 0:
                        P.act(hT[:, k, t * 512:(t + 1) * 512], tm, AF.Identity,
                              bias=modT[:, 3 * i * 8 + k, v:v + 1], scale=sclT[:, i, k, v:v + 1])
                    else:
                        P.ts(hT[:, k, t * 512:(t + 1) * 512], tm, sclT[:, i, k, v:v + 1],
                             modT[:, 3 * i * 8 + k, v:v + 1], ALU.mult, ALU.add)

        def ffn(i, wi, wo):
            actT = arena(0, 22 * TOK, BF16).rearrange("p (j t) -> p j t", j=22)
            for b in range(11):
                so, sch = next_slot()
                sv = A.view(so, 8 * 512, BF16).rearrange("p (k c) -> p k c", k=8)
                P.dma("pool", sv[:, :, 0:256],
                      wi[:, b * 256:(b + 1) * 256].rearrange("(k p) c -> p k c", p=128), sch)
                P.dma("pool", sv[:, :, 256:512],
                      wi[:, DFF + b * 256:DFF + (b + 1) * 256].rearrange("(k p) c -> p k c", p=128), sch)
                for jj in range(2):
                    j = 2 * b + jj
                    for t in range(3):
                        pg = bank()
                        pu = bank()
                        for k in range(8):
                            P.mm(pg, sv[:, k, jj * 128:(jj + 1) * 128], hT[:, k, t * 512:(t + 1) * 512],
                                 start=(k == 0), stop=(k == 7))
                        for k in range(8):
                            P.mm(pu, sv[:, k, 256 + jj * 128:256 + (jj + 1) * 128],
                                 hT[:, k, t * 512:(t + 1) * 512], start=(k == 0), stop=(k == 7))
                        sgt = tmpbuf()
                        P.act(sgt, pg, AF.Silu)
                        P.tt(actT[:, j, t * 512:(t + 1) * 512], sgt, pu, ALU.mult)
                if i == 0 and mod_rest:
                    mod_next()
            for b in range(8):
                so, sch = next_slot()
                sv = A.view(so, 22 * 128, BF16).rearrange("p (k c) -> p k c", k=22)
                P.dma("pool", sv, wo[:, b * 128:(b + 1) * 128].rearrange("(k p) c -> p k c", p=128), sch)
                if i == 0 and mod_rest:
                    mod_next()
                for mm_ in range(1):
                    m = b
                    for t in range(3):
                        v = 0 if t == 0 else 1
                        bk = bank()
                        for kk in range(22):
                            P.mm(bk, sv[:, kk, mm_ * 128:(mm_ + 1) * 128], actT[:, kk, t * 512:(t + 1) * 512],
                                 start=(kk == 0), stop=(kk == 21))
                        xs = xT[:, m, t * 512:(t + 1) * 512]
                        P.stt(xs, bk, gatT[:, i, m, v:v + 1], xs, ALU.mult, ALU.add)

        SQ_OFF = None
        norm_mod(0, SQ_OFF)
        ffn(0, f1_in, f1_out)
        while mod_rest:
            mod_next()
        mod_derived(1)
        mod_derived(2)
        if stage >= 2:
            norm_mod(1, SQ_OFF)
            bank_mod[0] = 8
            D1TRI = [cst[:, 128:384], cst[:, 384:640]]
            TRI = [cst[:, 256:384], cst[:, 512:640]]
            D2 = [cst[:, 640:768], cst[:, 768:896]]
            D1TRIb = [cstb[:, 128:384], cstb[:, 384:640]]
            TRIb = [cstb[:, 256:384], cstb[:, 512:640]]
            D2b = [cstb[:, 640:768], cstb[:, 768:896]]

            class Rot:
                def __init__(self, items):
                    self.items, self.i = items, 0

                def next(self):
                    x_ = self.items[self.i % len(self.items)]
                    self.i += 1
                    return x_

            PSB = [ps[:, i_ * 512:(i_ + 1) * 512] for i_ in range(8)]
            MASKB = [triFb, triBb]
            LASTCOL = [255, 128]
            YINF = ps[:, 6 * 512:7 * 512]
            YINB = ps[:, 7 * 512:8 * 512]

            class Bump:
                def __init__(self, base, size):
                    self.base, self.size, self.off = base, size, 0

                def take(self, nelem, dt, align=512):
                    self.off = (self.off + align - 1) // align * align
                    o = self.base + self.off
                    self.off += nelem * _esize(dt)
                    assert self.off <= self.size, (self.off, self.size)
                    return A.view(o, nelem, dt)

            MA = Bump(arena0, ARENA)
            MM = Bump(o_rstd[0], 20480)
            och = chan("outstate")
            sch_ld = [chan("stld%d" % i) for i in range(2)]
            wch = [chan("wa2ch"), chan("bach")]
            bcT = MA.take(1360, F32)
            negm = [MA.take(512, BF16).rearrange("p (h t) -> p h t", h=4) for _ in range(2)]
            rowtab = MM.take(1360, F32)
            cg2 = chan("constgrp2", serial=False)
            for (lo, src) in ((0, gnw), (256, snw), (1280, dsk), (1296, dtb), (1328, alog)):
                n = src.shape[1]
                P.dma("sp", rowtab[0:1, lo:lo + n], src, cg2)
            for (lo, hi) in ((0, 512), (512, 1024), (1024, 1360)):
                bk = bank()
                P.mm(bk[:, 0:hi - lo], onesF[0:1, :], rowtab[0:1, lo:hi])
                P.copy(bcT[:, lo:hi], bk[:, 0:hi - lo])
            P.act(bcT[:, 1328:1360], bcT[:, 1328:1360], AF.Exp)
            P.ts(bcT[:, 1328:1360], bcT[:, 1328:1360], -1.0, None, ALU.mult)
            gnw_bc = bcT[:, 0:256]
            snw_bc = bcT[:, 256:1280]
            dsk_bc = bcT[:, 1280:1296]
            dtb_bc = bcT[:, 1296:1328]
            a_bc = bcT[:, 1328:1360]
            for d in range(2):
                P.ts(negm[d], TRI[d].unsqueeze(1).to_broadcast([128, 4, 128]), -1.0, 100000.0, ALU.add, ALU.mult)
            MA_base = MA.off
            MM.off = 0

            def load_slot(pieces):
                so, sch = next_slot()
                tot = sum(p.shape[1] for p in pieces)
                assert 8 * tot * 2 <= SLOT
                sv = A.view(so, 8 * tot, BF16).rearrange("p (k c) -> p k c", k=8)
                o = 0
                for p in pieces:
                    n = p.shape[1]
                    P.dma("pool", sv[:, :, o:o + n], p.rearrange("(k p) c -> p k c", p=128), sch)
                    o += n
                return sv

            def mixer_sg(T0, NT, seqs, v, is2d):
                NCK = NT // 128
                NTL = NT // 512
                MA.off = MA_base
                MM.off = 0
                dt = MA.take(NCK * 32, F32).rearrange("p (c h) -> p c h", h=32)
                dta = MA.take(NCK * 32, F32).rearrange("p (c h) -> p c h", h=32)
                dtah = MA.take(NCK * 32, BF16, align=256).rearrange("p (c h) -> p c h", h=32)
                dtal = MA.take(NCK * 32, BF16, align=256).rearrange("p (c h) -> p c h", h=32)
                MA_ssd0 = MA.off
                afT = [MA.take(NT, BF16), MA.take(NT, BF16)]
                wa2 = MA.take(1024, BF16)
                ba = MA.take(1024, BF16)
                P.dma("pool", wa2[0:16, :].rearrange("k (d c) -> k d c", d=2), w_a2.rearrange("d k c -> k d c"), wch[0])
                P.dma("pool", ba[0:1, :], b_a.rearrange("(o d) c -> o (d c)", o=1), wch[1])
                sv = load_slot([w_in[:, 3072:3104], w_in[:, 5664:5696]])
                for d in range(2):
                    for tl in range(NTL):
                        bk = bank()
                        for k in range(8):
                            P.mm(bk[0:16, :], sv[:, k, 16 * d:16 * d + 16], hT[:, k, T0 + tl * 512:T0 + (tl + 1) * 512],
                                 start=(k == 0), stop=(k == 7))
                        P.copy(afT[d][0:16, tl * 512:(tl + 1) * 512], bk[0:16, :])
                bk = bank()
                for c in range(NCK):
                    for k in range(8):
                        P.mm(bk[:, c * 32:(c + 1) * 32], hT[:, k, T0 + c * 128:T0 + (c + 1) * 128], sv[:, k, 32:64],
                             start=(k == 0), stop=(k == 7))
                P.tt(dt, bk[:, 0:NCK * 32].rearrange("p (c h) -> p c h", h=32),
                     dtb_bc.unsqueeze(1).to_broadcast([128, NCK, 32]), ALU.add)
                P.act(dt, dt, AF.Exp)
                P.act(dt, dt, AF.Ln, bias=1.0)
                P.tt(dta, dt, a_bc.unsqueeze(1).to_broadcast([128, NCK, 32]), ALU.mult)
                P.copy(dtah, dta)
                P.tt(dtal, dta, dtah, ALU.subtract)

                MA_sg = MA.off

                def outproj(kT_list, rows0, nk):
                    ncol = 512 if nk > 2 else 1024
                    for blk in range(1024 // ncol):
                        so, sch = next_slot()
                        svo = A.view(so, nk * ncol, BF16).rearrange("p (k c) -> p k c", k=nk)
                        P.dma("pool", svo, w_out[rows0:rows0 + nk * 128, blk * ncol:(blk + 1) * ncol]
                              .rearrange("(k p) c -> p k c", p=128), sch)
                        for mm_ in range(ncol // 128):
                            m = blk * (ncol // 128) + mm_
                            for tl in range(NTL):
                                bk = bank()
                                for kk in range(nk):
                                    P.mm(bk, svo[:, kk, mm_ * 128:(mm_ + 1) * 128],
                                         kT_list[kk][:, tl * 512:(tl + 1) * 512], start=(kk == 0), stop=(kk == nk - 1))
                                xs = xT[:, m, T0 + tl * 512:T0 + (tl + 1) * 512]
                                P.stt(xs, bk, gatT[:, 1, m, v:v + 1], xs, ALU.mult, ALU.add)

                NH = 4 if FLAGS["gla"] else 0
                MA.off = MA_sg
                MM.off = 0
                R1 = []
                R2 = []
                for par_ in range(2):
                    MA.off = (MA.off + 511) // 512 * 512
                    o_r1 = MA.base + MA.off
                    qT_ = MA.take(NT, BF16)
                    kT_ = MA.take(NT, BF16)
                    ktok_ = MA.take(NCK * 128, BF16).rearrange("p (c f) -> p c f", f=128)
                    R1.append((qT_, kT_, ktok_, A.view(o_r1, 2 * NT, BF16).rearrange("p (j t) -> p j t", j=2)))
                    vtok_ = MA.take(NCK * 256, BF16).rearrange("p (c f) -> p c f", f=256)
                    rs_ = MA.take(NCK * 256, BF16).rearrange("p (c f) -> p c f", f=256)
                    R2.append((vtok_, rs_))
                o_l = [None, None]
                lh = []
                ll = []
                for d in range(2):
                    MA.off = (MA.off + 511) // 512 * 512
                    o_l[d] = MA.base + MA.off
                    lh.append(MA.take(NCK * 128, BF16, align=2).rearrange("p (c f) -> p c f", f=128))
                    ll.append(MA.take(NCK * 128, BF16, align=2).rearrange("p (c f) -> p c f", f=128))
                c3 = lambda: MA.take(NCK * 128, BF16).rearrange("p (c f) -> p c f", f=128)
                qi = [c3(), c3()]
                ki = [c3(), c3()]
                qg = [c3(), c3()]
                kst = [c3(), c3()]
                comb = c3()
                Sstore = [A.view(o_l[d], NCK * 256, BF16).rearrange("p (c f) -> p c f", f=256) for d in range(2)]
                dec = MM.take(2 * NCK, F32, align=256).rearrange("p (d c) -> p d c", d=2)
                o_ph2 = MM.off
                T_e = [MM.take(512, F32) for _ in range(2)]
                T_l = [MM.take(512, F32).rearrange("p (c f) -> p c f", f=128) for _ in range(2)]
                T_Eqg = [MM.take(512, F32).rearrange("p (c f) -> p c f", f=256) for _ in range(2)]
                T_Ek = [MM.take(256, F32).rearrange("p (c f) -> p c f", f=128) for _ in range(2)]
                T_kd = [MM.take(512, F32).rearrange("p (c f) -> p c f", f=128) for _ in range(2)]
                MM.off = o_ph2
                T_cF = MM.take(512, F32).rearrange("p (c f) -> p c f", f=128)
                T_cB = MM.take(512, F32).rearrange("p (c f) -> p c f", f=128)
                Sst = [[MM.take(256, F32) for _ in range(2)] for _ in range(len(seqs))]
                T_on = [MM.take(512, F32).rearrange("p (c f) -> p c f", f=256) for _ in range(2)]
                T_og = [MM.take(512, BF16).rearrange("p (c f) -> p c f", f=256) for _ in range(2)]
                T_junk = MM.take(256, F32)
                T_sq = [MM.take(8, F32, align=512) for _ in range(2)]
                HB = Rot([PSB[5], PSB[6], PSB[7]])
                MB = Rot([PSB[0], PSB[1], PSB[2], PSB[3]])
                JUNK = PSB[4]
                NFILL = FLAGS.get("nfill", 2)

                def filler(n=1):
                    for _ in range(n):
                        P.mm(JUNK.rearrange("p (h t) -> p h t", h=4), identB, negm[0])


                T_th = [MM.take(256, F32) for _ in range(2)]
                thc = [0]

                def gen_proj_vr(h):
                    vtok, rs = R2[h % 2]
                    sB = load_slot([w_in[:, 1024 + h * 256:1024 + (h + 1) * 256],
                                    w_in[:, 2048 + h * 256:2048 + (h + 1) * 256]])
                    for c in range(NCK):
                        bk = HB.next()
                        for k in range(8):
                            P.mm(bk, hT[:, k, T0 + c * 128:T0 + (c + 1) * 128], sB[:, k, :],
                                 start=(k == 0), stop=(k == 7))
                        P.copy(vtok[:, c, :], bk[:, 0:256])
                        th = T_th[thc[0] % 2]
                        thc[0] += 1
                        P.act(th, bk[:, 256:512], AF.Tanh, scale=0.5)
                        P.ts(th, th, 1.0, 0.5, ALU.add, ALU.mult)
                        P.tt(rs[:, c, :], th, bk[:, 256:512], ALU.mult)
                        yield

                def gen_proj(h):
                    qT, kT, ktok, _ = R1[h % 2]
                    sA = load_slot([w_in[:, h * 128:(h + 1) * 128], w_in[:, 512 + h * 128:512 + (h + 1) * 128]])
                    for tl in range(NTL):
                        for (dst, c0, sc_) in ((qT, 0, 128.0 ** -0.5), (kT, 128, 1.0)):
                            bk = HB.next()
                            for k in range(8):
                                P.mm(bk, sA[:, k, c0:c0 + 128], hT[:, k, T0 + tl * 512:T0 + (tl + 1) * 512],
                                     start=(k == 0), stop=(k == 7))
                            P.act(dst[:, tl * 512:(tl + 1) * 512], bk, AF.Copy, scale=sc_)
                            yield
                    for c4 in range(NCK // 4):
                        bk = HB.next()
                        for cc in range(4):
                            c = c4 * 4 + cc
                            for k in range(8):
                                P.mm(bk[:, cc * 128:(cc + 1) * 128], hT[:, k, T0 + c * 128:T0 + (c + 1) * 128],
                                     sA[:, k, 128:256], start=(k == 0), stop=(k == 7))
                        P.copy(ktok[:, c4 * 4:c4 * 4 + 4, :], bk.rearrange("p (c f) -> p c f", f=128))
                        yield

                def gen_outproj(h):
                    oT = R1[h % 2][3]
                    so, sch = next_slot()
                    svo = A.view(so, 2 * 1024, BF16).rearrange("p (k c) -> p k c", k=2)
                    P.dma("pool", svo, w_out[h * 256:(h + 1) * 256, :].rearrange("(k p) c -> p k c", p=128), sch)
                    for m in range(8):
                        for tl in range(NTL):
                            bk = HB.next()
                            for kk in range(2):
                                P.mm(bk, svo[:, kk, m * 128:(m + 1) * 128], oT[:, kk, tl * 512:(tl + 1) * 512],
                                     start=(kk == 0), stop=(kk == 1))
                            xs = xT[:, m, T0 + tl * 512:T0 + (tl + 1) * 512]
                            P.stt(xs, bk, gatT[:, 1, m, v:v + 1], xs, ALU.mult, ALU.add)
                            yield

                def gen_main(h):
                    qT, kT, ktok, oT = R1[h % 2]
                    vtok, rs = R2[h % 2]
                    qT3 = qT.rearrange("p (c t) -> p c t", t=128)
                    kT3 = kT.rearrange("p (c t) -> p c t", t=128)
                    for d in range(2):
                        for c4 in range(NCK // 4):
                            bk = MB.next()
                            for cc in range(4):
                                c = c4 * 4 + cc
                                P.mm(bk[:, cc * 128:(cc + 1) * 128], afT[d][0:16, c * 128:(c + 1) * 128],
                                     wa2[0:16, d * 512 + h * 128:d * 512 + (h + 1) * 128], start=True, stop=False)
                                P.mm(bk[:, cc * 128:(cc + 1) * 128], onesB[0:1, :],
                                     ba[0:1, d * 512 + h * 128:d * 512 + (h + 1) * 128], start=False, stop=True)
                            e = T_e[(d * (NCK // 4) + c4) % 2]
                            lf = T_l[(d * (NCK // 4) + c4) % 2]
                            P.act(e, bk, AF.Exp, scale=-1.0)
                            P.act(lf, e.rearrange("p (c f) -> p c f", f=128), AF.Ln, bias=1.0)
                            P.copy(lh[d][:, c4 * 4:c4 * 4 + 4, :], lf)
                            P.tt(ll[d][:, c4 * 4:c4 * 4 + 4, :], lf, lh[d][:, c4 * 4:c4 * 4 + 4, :], ALU.subtract)
                            filler(NFILL)
                            yield "ln"
                    it = 0
                    for d in range(2):
                        for c2 in range(NCK // 2):
                            bk = MB.next()
                            bk3 = bk.rearrange("p (c f) -> p c f", f=256)
                            for cc in range(2):
                                P.mm(bk3[:, cc, :], lh[d][:, c2 * 2 + cc, :], D1TRIb[d], start=True, stop=False)
                                P.mm(bk3[:, cc, :], ll[d][:, c2 * 2 + cc, :], D1TRIb[d], start=False, stop=True)
                            Eqg = T_Eqg[it % 2]
                            Ek = T_Ek[it % 2]
                            it += 1
                            P.act(Eqg, bk3, AF.Exp, scale=-1.0 / 16)
                            P.act(Ek, bk3[:, :, 0:128], AF.Exp, scale=1.0 / 16)
                            cs_ = slice(c2 * 2, c2 * 2 + 2)
                            P.tt(qi[d][:, cs_, :], qT3[:, cs_, :], Eqg[:, :, 0:128], ALU.mult)
                            P.tt(ki[d][:, cs_, :], kT3[:, cs_, :], Ek, ALU.mult)
                            P.tt(qg[d][:, cs_, :], qT3[:, cs_, :], Eqg[:, :, 128:256], ALU.mult)
                            P.copy(dec[:, d, cs_], Eqg[:, :, LASTCOL[d]], eng="act")
                            filler(NFILL)
                            yield "exp"
                    it = 0
                    for d in range(2):
                        for c4 in range(NCK // 4):
                            bk = MB.next()
                            P.mm(bk.rearrange("p (c f) -> p c f", f=128), D2b[d], lh[d][:, c4 * 4:c4 * 4 + 4, :],
                                 start=True, stop=False)
                            P.mm(bk.rearrange("p (c f) -> p c f", f=128), D2b[d], ll[d][:, c4 * 4:c4 * 4 + 4, :],
                                 start=False, stop=True)
                            kd = T_kd[it % 2]
                            it += 1
                            P.act(kd, bk.rearrange("p (c f) -> p c f", f=128), AF.Exp, scale=-1.0 / 16)
                            P.tt(kst[d][:, c4 * 4:c4 * 4 + 4, :], ktok[:, c4 * 4:c4 * 4 + 4, :], kd, ALU.mult)
                            filler(NFILL)
                            yield "exp"
                    for c4 in range(NCK // 4):
                        bF = MB.next()
                        bB = MB.next()
                        for cc in range(4):
                            c = c4 * 4 + cc
                            P.mm(bF[:, cc * 128:(cc + 1) * 128], ki[0][:, c, :], qi[0][:, c, :])
                            P.mm(bB[:, cc * 128:(cc + 1) * 128], ki[1][:, c, :], qi[1][:, c, :])
                        P.tt(T_cF, bF.rearrange("p (c f) -> p c f", f=128),
                             MASKB[0].unsqueeze(1).to_broadcast([128, 4, 128]), ALU.mult)
                        P.tt(T_cB, bB.rearrange("p (c f) -> p c f", f=128),
                             MASKB[1].unsqueeze(1).to_broadcast([128, 4, 128]), ALU.mult)
                        P.tt(comb[:, c4 * 4:c4 * 4 + 4, :], T_cF, T_cB, ALU.add)
                        filler(NFILL)
                        yield "exp"
                    for si, (cs, init, oidx) in enumerate(seqs):
                        for d in range(2):
                            if init == "zero":
                                P.memset(Sst[si][d], 0.0, eng="dve")
                            else:
                                P.dma("sp", Sst[si][d], sg_in[d, h], sch_ld[d])
                    nstep = max(len(cs) for cs, _, _ in seqs)
                    for i in range(nstep):
                        for si, (cs, init, oidx) in enumerate(seqs):
                            for d in range(2):
                                c = cs[i] if d == 0 else cs[-1 - i]
                                S = Sst[si][d]
                                P.copy(Sstore[d][:, c, :], S, eng="act")
                                if init != "zero" and i == len(cs) - 1:
                                    continue
                                bk = MB.next()
                                P.mm(bk[:, 0:256], kst[d][:, c, :], vtok[:, c, :])
                                P.stt(S, S, dec[:, d, c:c + 1], bk[:, 0:256], ALU.mult, ALU.add)
                        filler(NFILL)
                        yield "exp"
                    for si, (cs, init, oidx) in enumerate(seqs):
                        if oidx is not None:
                            for d in range(2):
                                P.dma("sp", nsg_out[oidx, d, h], Sst[si][d], och)
                    oT4 = oT.rearrange("p j (c t) -> p j c t", t=128)
                    PO = Rot([PSB[0], PSB[1]])
                    BT_ = Rot([PSB[2], PSB[3]])
                    pos = {}

                    def g5_A(c2):
                        po = PO.next()
                        po3 = po.rearrange("p (c f) -> p c f", f=256)
                        pos[c2] = po3
                        for cc in range(2):
                            c = c2 * 2 + cc
                            P.mm(po3[:, cc, :], comb[:, c, :], vtok[:, c, :], start=True, stop=False)
                            P.mm(po3[:, cc, :], qg[0][:, c, :], Sstore[0][:, c, :], start=False, stop=False)
                            P.mm(po3[:, cc, :], qg[1][:, c, :], Sstore[1][:, c, :], start=False, stop=True)

                    def g5_B(c2):
                        par = c2 % 2
                        po3 = pos[c2]
                        sq = T_sq[par]
                        for cc in range(2):
                            P.act(T_junk, po3[:, cc, :], AF.Square, accum_out=sq[:, cc:cc + 1])
                        P.act(sq[:, 2:4], sq[:, 0:2], AF.Ln, bias=EPS, scale=1.0 / 256)
                        P.act(sq[:, 4:6], sq[:, 2:4], AF.Exp, scale=-0.5)
                        on = T_on[par]
                        for cc in range(2):
                            P.stt(on[:, cc, :], po3[:, cc, :], sq[:, 4 + cc:5 + cc], gnw_bc, ALU.mult, ALU.mult)
                        og = T_og[par]
                        P.tt(og, on, rs[:, c2 * 2:c2 * 2 + 2, :], ALU.mult)
                        bt = BT_.next().bitcast(BF16)
                        for j in range(2):
                            for cc in range(2):
                                P.transpose(bt[:, (j * 2 + cc) * 128:(j * 2 + cc + 1) * 128],
                                            og[:, cc, j * 128:(j + 1) * 128], identB)
                        P.copy(oT4[:, :, c2 * 2:c2 * 2 + 2, :],
                               bt[:, 0:512].rearrange("p (j c t) -> p j c t", j=2, c=2), eng="act")

                    n2 = NCK // 2
                    g5_A(0)
                    for c2 in range(n2):
                        if c2 + 1 < n2:
                            g5_A(c2 + 1)
                        g5_B(c2)
                        filler(NFILL)
                        yield "ln"

                def drain(gen):
                    for _ in gen:
                        pass

                def chain(*gens):
                    for g_ in gens:
                        for _ in g_:
                            yield

                def step(gen):
                    try:
                        next(gen)
                        return True
                    except StopIteration:
                        return False

                if NH:
                    drain(gen_proj(0))
                    drain(gen_proj_vr(0))
                for h in range(NH):
                    heavy = []
                    if h >= 1:
                        heavy.append(gen_outproj(h - 1))
                    if h + 1 < NH:
                        heavy.append(gen_proj(h + 1))
                    q_plain = chain(*heavy)
                    q_tanh = gen_proj_vr(h + 1) if h + 1 < NH else iter(())
                    a_plain = a_tanh = True
                    for tag in gen_main(h):
                        if tag == "exp" and a_tanh:
                            a_tanh = step(q_tanh)
                            if a_tanh:
                                continue
                        if a_plain:
                            a_plain = step(q_plain)
                    if a_plain:
                        drain(q_plain)
                    if a_tanh:
                        drain(q_tanh)
                if NH:
                    drain(gen_outproj(NH - 1))

                MA.off = MA_ssd0
                yzk = [MA.take(NCK * 512, BF16).rearrange("p (c f) -> p c f", f=512) for _ in range(2)]
                ssacc = MA.take(2 * NCK, F32, align=256)
                MA_ssd = MA.off
                PADN = 1188 if is2d else 516
                if not FLAGS["ssd"]:
                    return
                for g in range(2):
                    bank_mod[0] = 8
                    MA.off = MA_ssd
                    MM.off = 0
                    x_tok = MA.take(NCK * 512, BF16).rearrange("p (c f) -> p c f", f=512)
                    B_tok = MA.take(NCK * 128, BF16).rearrange("p (c f) -> p c f", f=128)
                    xcBC = MA.take(2 * NT, BF16).rearrange("p (j t) -> p j t", j=2)
                    MA_scan = MA.off
                    xcX = MA.take(4 * NT, BF16).rearrange("p (j t) -> p j t", j=4)
                    padv = MA.take(6 * PADN, BF16).rearrange("p (j t) -> p j t", j=6)
                    Sst = [[MM.take(512, F32) for _ in range(2)] for _ in range(len(seqs))]
                    SM = MM.take(2 * 4 * NCK * 8, F32).rearrange("p (d k c h) -> p d k c h", d=2, k=4, c=NCK)
                    MM_ph = MM.off
                    diag2 = [[MM.take(128, BF16, align=256) for _ in range(9)] for _ in range(2)]
                    P.memset(padv, 0.0, eng="dve")
                    sX = load_slot([w_in[:, 4128 + g * 512:4128 + (g + 1) * 512]])
                    sBC = load_slot([w_in[:, 5152 + g * 128:5152 + (g + 1) * 128],
                                     w_in[:, 5408 + g * 128:5408 + (g + 1) * 128]])

                    def xc(j):
                        return xcX[:, j, :] if j < 4 else xcBC[:, j - 4, :]

                    def padint(j, tl, sh=(0, 0)):
                        if is2d:
                            pv = padv[:, j, :].rearrange("p (r w) -> p r w", r=18)
                            return pv[:, 8 * tl + 1 + sh[0]:8 * tl + 9 + sh[0], 1 + sh[1]:65 + sh[1]]
                        pv = padv[:, j, :].rearrange("p (s w) -> p s w", s=2)
                        return pv[:, :, 1 + sh[1]:257 + sh[1]]

                    def as3(ap2):
                        return ap2.rearrange("p (r w) -> p r w", r=(8 if is2d else 2))

                    for j in range(6):
                        for tl in range(NTL):
                            bk = bank()
                            for k in range(8):
                                lw_ = sX[:, k, j * 128:(j + 1) * 128] if j < 4 else sBC[:, k, (j - 4) * 128:(j - 3) * 128]
                                P.mm(bk, lw_, hT[:, k, T0 + tl * 512:T0 + (tl + 1) * 512], start=(k == 0), stop=(k == 7))
                            P.copy(padint(j, tl), as3(bk), eng="act")
                    taps = [(ky, kx) for ky in range(3) for kx in range(3)] if is2d else [(1, kx) for kx in range(3)]
                    for j in range(6):
                        ci = (g * 4 + j) if j < 4 else (8 + g if j == 4 else 10 + g)
                        diag = diag2[j % 2]
                        for ti, (ky, kx) in enumerate(taps):
                            tap = ky * 3 + kx
                            P.ts(diag[ti], identB, colsB[:, tap * 12 + ci:tap * 12 + ci + 1], None, ALU.mult)
                        for tl in range(NTL):
                            bk = bank()
                            for ti, (ky, kx) in enumerate(taps):
                                P.mm(as3(bk), diag[ti], padint(j, tl, (ky - 1, kx - 1)),
                                     start=(ti == 0), stop=(ti == len(taps) - 1))
                            P.act(xc(j)[:, tl * 512:(tl + 1) * 512], bk, AF.Silu, bias=colsB[:, 108 + ci:109 + ci])
                    for c in range(NCK):
                        bt = bank().bitcast(BF16)
                        for j in range(4):
                            P.transpose(bt[:, j * 128:(j + 1) * 128], xcX[:, j, c * 128:(c + 1) * 128], identB)
                        P.copy(x_tok[:, c, :], bt[:, 0:512], eng=("act" if c % 2 else "dve"))
                    for c4 in range(NCK // 4):
                        bt = bank().bitcast(BF16)
                        for cc in range(4):
                            c = c4 * 4 + cc
                            P.transpose(bt[:, cc * 128:(cc + 1) * 128], xcBC[:, 0, c * 128:(c + 1) * 128], identB)
                        P.copy(B_tok[:, c4 * 4:c4 * 4 + 4, :], bt[:, 0:512].rearrange("p (c f) -> p c f", f=128))
                    BT = xcBC[:, 0, :]
                    CT = xcBC[:, 1, :]
                    sZ = load_slot([w_in[:, 3104 + g * 512:3104 + (g + 1) * 512]])
                    MA.off = MA_scan
                    Sstore = [MA.take(NCK * 512, BF16).rearrange("p (c f) -> p c f", f=512) for _ in range(2)]
                    CB_all = MA.take(NCK * 128, F32).rearrange("p (c f) -> p c f", f=128)
                    T_t1 = [MA.take(512, F32) for _ in range(2)]
                    T_t2 = MA.take(512, F32)
                    T_zs = MA.take(512, F32)
                    MM.off = MM_ph
                    T_expL = [MM.take(512, F32).rearrange("p (h t) -> p h t", h=4) for _ in range(2)]
                    T_scr = [MM.take(512, BF16).rearrange("p (h t) -> p h t", h=4) for _ in range(4)]
                    T_xw = [MM.take(512, BF16) for _ in range(2)]
                    T_dsk = [(MM if is2d else MA).take(128, BF16, align=256) for _ in range(8)]
                    T_t3 = MM.take(512, F32) if is2d else MA.take(512, F32)
                    v3 = lambda a: a.rearrange("p (h q) -> p h q", h=8)
                    bc3 = lambda a: a.unsqueeze(2).to_broadcast([128, 8, 64])
                    n8 = NCK * 8
                    for d in range(2):
                        cl = slice(d * 16 + 8 * g, d * 16 + 8 * g + 8)
                        bk = bank()
                        o3 = lambda k_: bk[:, k_ * n8:(k_ + 1) * n8].rearrange("p (c h) -> p c h", h=8)
                        for k_, L_ in enumerate((TRIb[d], D2b[d], onesB)):
                            P.mm(o3(k_), L_, dtah[:, :, cl], start=True, stop=False)
                            P.mm(o3(k_), L_, dtal[:, :, cl], start=False, stop=True)
                        P.ts(SM[:, d, 0], dt[:, :, cl], 1e-18, None, ALU.max)
                        P.act(SM[:, d, 0], SM[:, d, 0], AF.Ln)
                        P.stt(SM[:, d, 0], o3(0), -1.0, SM[:, d, 0], ALU.mult, ALU.add)
                        P.act(SM[:, d, 1], o3(0), AF.Exp)
                        P.act(SM[:, d, 2], o3(1), AF.Exp)
                        P.tt(SM[:, d, 2], SM[:, d, 2], dt[:, :, cl], ALU.mult)
                        P.act(SM[:, d, 3], o3(2), AF.Exp)
                    for c4 in range(NCK // 4):
                        bk = bank()
                        for cc in range(4):
                            c = c4 * 4 + cc
                            P.mm(bk[:, cc * 128:(cc + 1) * 128], BT[:, c * 128:(c + 1) * 128], CT[:, c * 128:(c + 1) * 128])
                        P.copy(CB_all[:, c4 * 4:c4 * 4 + 4, :], bk.rearrange("p (c f) -> p c f", f=128), eng="act")
                    for si, (cs, init, oidx) in enumerate(seqs):
                        for d in range(2):
                            S = Sst[si][d]
                            if init == "zero":
                                P.memset(S, 0.0, eng="dve")
                            else:
                                stg = T_zs.rearrange("p (q n) -> p q n", q=4)
                                for q in range(4):
                                    P.dma("sp", stg[:, q, :],
                                          ss_in[d, 8 * g + 2 * q:8 * g + 2 * q + 2].rearrange("h p n -> (h p) n"),
                                          sch_ld[d])
                                bk = bank()
                                for q in range(4):
                                    P.transpose(bk[:, q * 128:(q + 1) * 128], stg[:, q, :], identF)
                                P.copy(S, bk)
                    nstep = max(len(cs) for cs, _, _ in seqs)
                    rxc = [0]
                    MISC = Rot([PSB[6], PSB[7]])

                    def rec_step(i):
                        for si, (cs, init, oidx) in enumerate(seqs):
                            for d in range(2):
                                c = cs[i] if d == 0 else cs[-1 - i]
                                S = Sst[si][d]
                                P.copy(Sstore[d][:, c, :], S, eng="act")
                                if init != "zero" and i == len(cs) - 1:
                                    continue
                                xw = T_xw[rxc[0] % 2]
                                rxc[0] += 1
                                P.tt(v3(xw), v3(x_tok[:, c, :]), bc3(SM[:, d, 2, c, :]), ALU.mult, eng="dve")
                                bk = MISC.next()
                                P.mm(bk, B_tok[:, c, :], xw)
                                P.tt(v3(S), v3(S), bc3(SM[:, d, 3, c, :]), ALU.mult, eng="dve")
                                P.tt(S, S, bk, ALU.add)

                    def rec_finish():
                        for si, (cs, init, oidx) in enumerate(seqs):
                            if oidx is not None:
                                for d in range(2):
                                    S = Sst[si][d]
                                    bk = MISC.next()
                                    for q in range(4):
                                        P.transpose(bk[:, q * 128:(q + 1) * 128], S[:, q * 128:(q + 1) * 128], identF)
                                    stg = T_zs.rearrange("p (q n) -> p q n", q=4)
                                    P.copy(stg, bk.rearrange("p (q n) -> p q n", q=4))
                                    for q in range(4):
                                        P.dma("sp", nss_out[oidx, d, 8 * g + 2 * q:8 * g + 2 * q + 2].rearrange("h p n -> (h p) n"),
                                              stg[:, q, :], och)

                    BBR = Rot([PSB[0], PSB[1], PSB[2], PSB[3]]) if not FLAGS.get("sfill", 0) else Rot([PSB[0], PSB[1], PSB[2]])
                    YIN1 = [PSB[4], PSB[5]]
                    rot = [0]
                    units = [(c, half) for c in range(NCK) for half in range(2)]
                    bbs = {}
                    for hh in range(8):
                        P.act(T_dsk[hh], identB, AF.Copy, scale=dsk_bc[:, 8 * g + hh:8 * g + hh + 1])

                    def st_A(u):
                        c, half = u
                        for d in range(2):
                            c0 = d * 16 + 8 * g + 4 * half
                            bb = BBR.next()
                            bbs[(u, d)] = bb
                            P.mm(bb.rearrange("p (h t) -> p h t", h=4), identB, negm[d], start=True, stop=False)
                            for h4 in range(4):
                                o_ = bb[:, h4 * 128:(h4 + 1) * 128]
                                P.mm(o_, dtah[:, c, c0 + h4:c0 + h4 + 1].to_broadcast([128, 128]), TRIb[d],
                                     start=False, stop=False)
                                P.mm(o_, dtal[:, c, c0 + h4:c0 + h4 + 1].to_broadcast([128, 128]), TRIb[d],
                                     start=False, stop=(h4 == 3))

                    ypend = []

                    def flush_y():
                        while ypend:
                            o_, s0_, s1_, dk_, xh_ = ypend.pop(0)
                            P.mm(o_, s0_, xh_, start=True, stop=False)
                            P.mm(o_, s1_, xh_, start=False, stop=False)
                            P.mm(o_, dk_, xh_, start=False, stop=True)

                    def st_B(u):
                        c, half = u
                        yin = YIN1[c % 2]
                        scs = []
                        for d in range(2):
                            bb = bbs[(u, d)]
                            eL = T_expL[rot[0] % 2]
                            sc_ = T_scr[rot[0] % 4]
                            rot[0] += 1
                            for h4 in range(4):
                                hh = half * 4 + h4
                                P.act(eL[:, h4, :], bb[:, h4 * 128:(h4 + 1) * 128], AF.Exp, bias=SM[:, d, 0, c, hh:hh + 1])
                            P.tt(sc_, eL, CB_all[:, c, :].unsqueeze(1).to_broadcast([128, 4, 128]), ALU.mult)
                            scs.append(sc_)
                            step_pend()
                        flush_y()
                        for h4 in range(4):
                            hh = half * 4 + h4
                            xh = x_tok[:, c, hh * 64:(hh + 1) * 64]
                            ypend.append((yin[:, hh * 64:(hh + 1) * 64], scs[0][:, h4, :], scs[1][:, h4, :], T_dsk[hh], xh))
                        step_pend()

                    def st_C(c):
                        par = c % 2
                        gc = slice(c * 128, (c + 1) * 128)
                        t1 = T_t1[par]
                        P.copy(t1, YIN1[par], eng="act")
                        yield
                        byf = MISC.next()
                        P.mm(byf, CT[:, gc], Sstore[0][:, c, :])
                        byb = MISC.next()
                        P.mm(byb, CT[:, gc], Sstore[1][:, c, :])
                        yield
                        P.tt(v3(T_t2), v3(byf), bc3(SM[:, 0, 1, c, :]), ALU.mult)
                        P.tt(v3(T_t3), v3(byb), bc3(SM[:, 1, 1, c, :]), ALU.mult)
                        yield
                        bz = MISC.next()
                        for k in range(8):
                            P.mm(bz, hT[:, k, T0 + c * 128:T0 + (c + 1) * 128], sZ[:, k, :], start=(k == 0), stop=(k == 7))
                        P.tt(t1, t1, T_t2, ALU.add, eng="dve")
                        P.tt(t1, t1, T_t3, ALU.add, eng="dve")
                        yield
                        P.act(T_zs, bz, AF.Tanh, scale=0.5)
                        yield
                        P.stt(T_zs, T_zs, 1.0, bz, ALU.add, ALU.mult)
                        yield
                        P.stt(yzk[g][:, c, :], T_zs, 0.5, t1, ALU.mult, ALU.mult)
                        P.act(T_t2, yzk[g][:, c, :], AF.Square, accum_out=ssacc[:, g * NCK + c:g * NCK + c + 1])
                        yield

                    pend = []

                    def step_pend():
                        if pend:
                            try:
                                next(pend[0])
                            except StopIteration:
                                pend.pop(0)
                                step_pend()

                    st_A(units[0])
                    defer_c = []
                    for ui, u in enumerate(units):
                        if ui + 1 < len(units):
                            st_A(units[ui + 1])
                        st_B(u)
                        spu = (nstep + 3) // 4
                        for i_ in range(ui * spu, min((ui + 1) * spu, nstep)):
                            rec_step(i_)
                            if i_ == nstep - 1:
                                rec_finish()
                        if u[1] == 1:
                            defer_c.append(u[0])
                        if ui >= 3 and ui % 2 == 1:
                            flush_y()
                            for c_ in defer_c:
                                gen_ = st_C(c_)
                                next(gen_)
                                pend.append(gen_)
                            defer_c = []
                    flush_y()
                    while pend:
                        step_pend()
                MA.off = MA_ssd
                MM.off = 0
                yT = MA.take(8 * NT, BF16).rearrange("p (j t) -> p j t", j=8)
                sst = MM.take(NCK, F32, align=256)
                rsd = MM.take(NCK, F32, align=256)
                T_yn = [MM.take(512, BF16) for _ in range(2)]
                P.tt(sst, ssacc[:, 0:NCK], ssacc[:, NCK:2 * NCK], ALU.add)
                P.act(sst, sst, AF.Ln, bias=EPS, scale=1.0 / 1024)
                P.act(rsd, sst, AF.Exp, scale=-0.5)
                for c in range(NCK):
                    for g in range(2):
                        yn = T_yn[g]
                        P.stt(yn, yzk[g][:, c, :], rsd[:, c:c + 1], snw_bc[:, g * 512:(g + 1) * 512], ALU.mult, ALU.mult)
                        bt = bank().bitcast(BF16)
                        for j in range(4):
                            P.transpose(bt[:, j * 128:(j + 1) * 128], yn[:, j * 128:(j + 1) * 128], identB)
                        P.copy(yT[:, g * 4:g * 4 + 4, c * 128:(c + 1) * 128],
                               bt[:, 0:512].rearrange("p (j t) -> p j t", j=4), eng=("act" if g else "dve"))
                outproj([yT[:, j, :] for j in range(8)], 1024, 8)

            if FLAGS["sgp"]:
                mixer_sg(0, 512, [([0, 1], "zero", 0), ([2, 3], "zero", 1)], 0, False)
            if FLAGS["sgs"]:
                mixer_sg(512, 1024, [(list(range(8)), "load", None)], 1, True)
            bank_mod[0] = 8

        FOFF = 22 * TOK * 2
        rowF = A.view(o_stage[0], 1024, F32)
        fnw_bc = arena(FOFF, 1024, F32)
        fin_junk = A.view(o_tmp[0], 512, F32)
        fin_small = [A.view(o_rstd[i_], 8, F32) for i_ in range(2)]
        fch = chan("fnrow")
        P.dma("sp", rowF[0:1, :], rt[112:120, :].rearrange("(o r) c -> o (r c)", o=1), fch)
        for hf in range(2):
            bk = bank()
            P.mm(bk, onesF[0:1, :], rowF[0:1, hf * 512:(hf + 1) * 512])
            P.copy(fnw_bc[:, hf * 512:(hf + 1) * 512], bk)
        norm_mod(2, SQ_OFF)
        ffn(2, f2_in, f2_out)
        for c in range(NCH):
            st = A.view(o_stage[c % 2], 1024, F32)
            sm_ = fin_small[c % 2]
            bks = []
            for half in range(2):
                bk = bank()
                bks.append(bk)
                for kk in range(4):
                    k = half * 4 + kk
                    P.transpose(bk[:, kk * 128:(kk + 1) * 128], xT[:, k, c * 128:(c + 1) * 128], identF)
                P.act(fin_junk, bk, AF.Square, accum_out=sm_[:, half:half + 1])
            P.tt(sm_[:, 2:3], sm_[:, 0:1], sm_[:, 1:2], ALU.add)
            P.act(sm_[:, 3:4], sm_[:, 2:3], AF.Ln, bias=EPS, scale=1.0 / D)
            P.act(sm_[:, 4:5], sm_[:, 3:4], AF.Exp, scale=-0.5)
            for half in range(2):
                P.stt(st[:, half * 512:(half + 1) * 512], bks[half], sm_[:, 4:5],
                      fnw_bc[:, half * 512:(half + 1) * 512], ALU.mult, ALU.mult)
            P.dma("sp", y_out[c * 128:(c + 1) * 128, :], st, stage_ch[c % 2])

        P.emit(esem)
    return nc


def make_in_maps(inputs):
    f = lambda a: np.ascontiguousarray(np.asarray(a, dtype=np.float32))
    xp = f(inputs["x_prompt"])
    xs = f(inputs["x_sample"])
    consts = host_consts()
    shared = {
        "consts": consts,
        "w_mod": f(inputs["w_mod"][0]),
        "ffn1_w_in": f(inputs["ffn1_w_in"][0]), "ffn1_w_out": f(inputs["ffn1_w_out"][0]),
        "ffn2_w_in": f(inputs["ffn2_w_in"][0]), "ffn2_w_out": f(inputs["ffn2_w_out"][0]),
        "w_in": f(inputs["w_in"][0]), "w_out": f(inputs["w_out"][0]),
        "gla_w_a2": f(inputs["gla_w_a2"][0]), "gla_b_a": f(inputs["gla_b_a"][0]),
        "gla_norm_w": f(inputs["gla_norm_w"][0]).reshape(1, 256),
        "dt_bias": f(inputs["dt_bias"][0]).reshape(1, 32),
        "a_log": f(inputs["a_log"][0]).reshape(1, 32),
        "d_skip": f(inputs["d_skip"][0]).reshape(1, 16),
        "ssd_norm_w": f(inputs["ssd_norm_w"][0]).reshape(1, 1024),
    }
    maps = []
    for i in range(8):
        rt = np.concatenate([
            f(inputs["b_mod"][0]).reshape(72, 128),
            f(inputs["c_ctx"]).reshape(8, 128),
            f(inputs["c"][i]).reshape(8, 128),
            f(inputs["norm_ffn1"][0]).reshape(8, 128),
            f(inputs["norm_mix"][0]).reshape(8, 128),
            f(inputs["norm_ffn2"][0]).reshape(8, 128),
            f(inputs["final_norm"]).reshape(8, 128),
            f(inputs["conv_w"][0]).reshape(108, 128),
            f(inputs["conv_b"][0]).reshape(12, 128),
        ], axis=0)
        m = dict(shared)
        m["xin"] = np.concatenate([xp[2 * i], xp[2 * i + 1], xs[i]], axis=0)
        m["rt"] = np.ascontiguousarray(rt)
        m["sg"] = f(inputs["state_gla"][i, 0])
        m["ss"] = f(inputs["state_ssd"][i, 0])
        maps.append(m)
    return maps


_NC_CACHE = {}


def kernel(**inputs):
    maps = make_in_maps(inputs)
    if "nc" not in _NC_CACHE:
        _NC_CACHE["nc"] = build()
    nc = _NC_CACHE["nc"]
    res = run_bass_kernel_spmd(nc, maps, core_ids=list(range(8)))
    r = res.results
    odt = np.asarray(r[0]["y"]).dtype
    y_prompt = np.zeros((16, 256, D), odt)
    y_sample = np.zeros((8, 1024, D), odt)
    nsg = np.zeros((16, 1, 2, 4, 128, 256), odt)
    nss = np.zeros((16, 1, 2, 16, 64, 128), odt)
    for i in range(8):
        y = np.asarray(r[i]["y"])
        y_prompt[2 * i] = y[0:256]
        y_prompt[2 * i + 1] = y[256:512]
        y_sample[i] = y[512:]
        nsg[2 * i:2 * i + 2, 0] = np.asarray(r[i]["nsg"])
        nss[2 * i:2 * i + 2, 0] = np.asarray(r[i]["nss"])
    return (y_prompt, y_sample, nsg, nss)
```

```python
import contextlib
import numpy as np
import concourse.bass as bass
import concourse.mybir as mybir
from concourse.bass_utils import run_bass_kernel_spmd

F32 = mybir.dt.float32
BF16 = mybir.dt.bfloat16
AF = mybir.ActivationFunctionType
ALU = mybir.AluOpType

D = 1024
TOK = 1536
NCH = 12
DFF = 2816
NIN = 5696
EPS = 1e-6
SB_BYTES = 206 * 1024
PAGE = 512
PS_PAGE = 2048
SAME_SYNC = True

ENGS = ["pe", "act", "dve", "pool", "sp"]


class Chan:
    def __init__(self, sem, serial=True):
        self.sem = sem
        self.n = 0
        self.serial = serial


class Op:
    __slots__ = ("fn", "deps", "chan", "signal", "extra")

    def __init__(self, fn, deps, chan):
        self.fn = fn
        self.deps = deps
        self.chan = chan
        self.signal = False
        self.extra = None


def _esize(dt):
    return 2 if dt == BF16 else 4


class Prog:
    def __init__(self, nc, sbname, psname):
        self.nc = nc
        self.sbname = sbname
        self.psname = psname
        self.ops = {e: [] for e in ENGS}
        self.pages = {}
        self.seen = {e: {} for e in ENGS}
        self.chans = []

    def pages_of(self, ap):
        name = ap.tensor.name
        if name == self.sbname:
            row, pg, tag = SB_BYTES, PAGE, "s"
        elif name == self.psname:
            row, pg, tag = 16384, PS_PAGE, "p"
        else:
            return []
        es = _esize(ap.dtype)
        lo = (int(ap.offset) * es) % row
        ext = 1
        for (s, c) in ap.ap[1:]:
            ext += (c - 1) * abs(s)
        hi = lo + ext * es
        assert hi <= row, (lo, hi, ap)
        return [(tag, i) for i in range(lo // pg, (hi - 1) // pg + 1)]

    def record(self, eng, fn, reads, writes, chan=None):
        if chan is not None:
            chan.n += 1
            me = ("d", chan, chan.n)
        else:
            me = ("e", eng, len(self.ops[eng]))
        deps = {}

        def add(dep):
            k = (dep[0], dep[1])
            if deps.get(k, -1) < dep[2]:
                deps[k] = dep[2]

        rp = []
        for ap in reads:
            rp += self.pages_of(ap)
        wp = []
        for ap in writes:
            wp += self.pages_of(ap)
        for pg in rp:
            st = self.pages.get(pg)
            if st and st[0]:
                add(st[0])
            if st and pg[0] == "p":
                for rk, r in st[1].items():
                    if rk != (me[0], me[1]):
                        add(r)
        for pg in wp:
            st = self.pages.get(pg)
            if st:
                if st[0]:
                    add(st[0])
                for r in st[1].values():
                    add(r)
        if chan is not None and chan.serial and chan.n > 1:
            add(("d", chan, chan.n - 1))
        final = []
        for (kind, key), val in deps.items():
            if kind == "e" and key == eng:
                if eng in ("pe", "sp"):
                    continue
                if not SAME_SYNC and chan is None:
                    continue
            if kind == "d" and chan is not None and key is chan and not chan.serial:
                continue
            if self.seen[eng].get((kind, key), -1) >= val:
                continue
            self.seen[eng][(kind, key)] = val
            final.append((kind, key, val))
            if kind == "e":
                self.ops[key][val].signal = True
        for pg in rp:
            st = self.pages.setdefault(pg, [None, {}])
            st[1][(me[0], me[1])] = me
        for pg in wp:
            st = self.pages.setdefault(pg, [None, {}])
            st[0] = me
            st[1] = {}
        op = Op(fn, final, chan)
        self.ops[eng].append(op)
        return op

    def mm(self, out, lhsT, rhs, start=True, stop=True):
        self.record("pe", lambda e: e.matmul(out, lhsT, rhs, start=start, stop=stop),
                    [lhsT, rhs] + ([] if start else []), [out])

    def transpose(self, out, in_, ident):
        self.record("pe", lambda e: e.transpose(out, in_, ident), [in_, ident], [out])

    def act(self, out, in_, func, bias=None, scale=None, accum_out=None):
        reads = [in_]
        kw = {}
        writes = [out]
        if accum_out is not None:
            kw["accum_out"] = accum_out
            writes.append(accum_out)
        if bias is not None:
            kw["bias"] = bias
            if not isinstance(bias, (int, float)):
                reads.append(bias)
        if scale is not None:
            kw["scale"] = scale
            if not isinstance(scale, (int, float)):
                reads.append(scale)
        self.record("act", lambda e: e.activation(out, in_, func, **kw), reads, writes)

    def tt(self, out, in0, in1, op, eng="dve"):
        self.record(eng, lambda e: e.tensor_tensor(out, in0, in1, op), [in0, in1], [out])

    def ts(self, out, in0, s1, s2, op0, op1=None, eng="dve"):
        reads = [in0]
        if not isinstance(s1, (int, float)):
            reads.append(s1)
        if s2 is not None and not isinstance(s2, (int, float)):
            reads.append(s2)
        if op1 is None:
            self.record(eng, lambda e: e.tensor_scalar(out, in0, s1, None, op0), reads, [out])
        else:
            self.record(eng, lambda e: e.tensor_scalar(out, in0, s1, s2, op0, op1), reads, [out])

    def stt(self, out, in0, scalar, in1, op0, op1, eng="dve"):
        reads = [in0, in1]
        if not isinstance(scalar, (int, float)):
            reads.append(scalar)
        self.record(eng, lambda e: e.scalar_tensor_tensor(out, in0, scalar, in1, op0, op1),
                    reads, [out])

    def copy(self, out, in_, eng="dve"):
        if eng == "act":
            self.record("act", lambda e: e.copy(out, in_), [in_], [out])
        else:
            self.record(eng, lambda e: e.tensor_copy(out, in_), [in_], [out])

    def memset(self, out, val, eng="dve"):
        self.record(eng, lambda e: e.memset(out, val), [], [out])

    def dma(self, q, out, in_, chan):
        self.record(q, lambda e: e.dma_start(out=out, in_=in_), [in_], [out], chan=chan)

    def emit(self, esem):
        nc = self.nc
        sig = {}
        for e in ENGS:
            cnt = 0
            arr = []
            for op in self.ops[e]:
                if op.signal:
                    cnt += 1
                arr.append(cnt)
            sig[e] = arr
        chans = self.chans

        def run(name, eng):
            for op in self.ops[name]:
                for (kind, key, val) in op.deps:
                    if kind == "e":
                        eng.wait_ge(esem[key], sig[key][val])
                    else:
                        eng.wait_ge(key.sem, 16 * (val if key.serial else key.n))
                ins = op.fn(eng)
                if op.chan is not None:
                    ins.then_inc(op.chan.sem, 16)
                elif op.signal:
                    ins.then_inc(esem[name], 1)
            if name == "sp":
                for ch in chans:
                    if ch.n:
                        eng.wait_ge(ch.sem, 16 * ch.n)

        with nc.Block() as block:
            @block.tensor
            def _(e):
                run("pe", e)

            @block.scalar
            def _(e):
                run("act", e)

            @block.vector
            def _(e):
                run("dve", e)

            @block.gpsimd
            def _(e):
                run("pool", e)

            @block.sync
            def _(e):
                run("sp", e)


class Alloc:
    def __init__(self, sb):
        self.sb = sb
        self.off = 0

    def take(self, nbytes, align=PAGE):
        self.off = (self.off + align - 1) // align * align
        o = self.off
        self.off += nbytes
        assert self.off <= SB_BYTES, self.off
        return o

    def view(self, off, nelem, dt):
        a = self.sb[:, off // 4:(off + nelem * _esize(dt)) // 4]
        if dt != F32:
            a = a.bitcast(dt)
        return a


NCONST = 8


def host_consts():
    i = np.arange(128)
    u = i[:, None]
    t = i[None, :]
    ref = 64
    ident = (u == t)
    triF = (u <= t)
    triB = (u >= t)
    d1F = (u <= t).astype(np.float32) - (u <= ref).astype(np.float32)
    d1B = (u >= t).astype(np.float32) - (u >= ref).astype(np.float32)
    d2F = (u > t)
    d2B = (u < t)
    ones = np.ones((128, 128))
    return np.concatenate([np.asarray(m, np.float32) for m in
                           (ident, d1F, triF, d1B, triB, d2F, d2B, ones)], axis=1)


import os
FLAGS = {"gla": True, "ssd": True, "sgp": True, "sgs": True, "gbwd": True, "gfwd": True, "gout": True, "gnorm": True}


def build(stage=2, debug=False):
    for k_ in FLAGS:
        if os.environ.get("MK_" + k_.upper()) is not None:
            FLAGS[k_] = os.environ["MK_" + k_.upper()] == "1"
    nc = bass.Bass("TRN2", target_bir_lowering=False)

    def din(name, shape):
        return nc.dram_tensor(name, list(shape), F32, kind="ExternalInput").ap()

    def dout(name, shape):
        return nc.dram_tensor(name, list(shape), F32, kind="ExternalOutput").ap()

    xin = din("xin", [TOK, D])
    rt = din("rt", [240, 128])
    consts_d = din("consts", [128, NCONST * 128])
    w_mod = din("w_mod", [D, 9 * D])
    f1_in = din("ffn1_w_in", [D, 2 * DFF])
    f1_out = din("ffn1_w_out", [DFF, D])
    f2_in = din("ffn2_w_in", [D, 2 * DFF])
    f2_out = din("ffn2_w_out", [DFF, D])
    w_in = din("w_in", [D, NIN])
    w_out = din("w_out", [2 * D, D])
    w_a2 = din("gla_w_a2", [2, 16, 512])
    b_a = din("gla_b_a", [2, 512])
    gnw = din("gla_norm_w", [1, 256])
    dtb = din("dt_bias", [1, 32])
    alog = din("a_log", [1, 32])
    dsk = din("d_skip", [1, 16])
    snw = din("ssd_norm_w", [1, 1024])
    sg_in = din("sg", [2, 4, 128, 256])
    ss_in = din("ss", [2, 16, 64, 128])
    y_out = dout("y", [TOK, D])
    nsg_out = dout("nsg", [2, 2, 4, 128, 256])
    nss_out = dout("nss", [2, 2, 16, 64, 128])
    dbg = {}

    with contextlib.ExitStack() as es:
        sb = es.enter_context(nc.sbuf_tensor("SB", [128, SB_BYTES // 4], F32))
        ps = es.enter_context(nc.psum_tensor("PS", [128, 4096], F32))
        P = Prog(nc, "SB", "PS")
        esem = {e: es.enter_context(nc.semaphore("s_" + e)) for e in ENGS}

        def chan(name, serial=True):
            c = Chan(es.enter_context(nc.semaphore(name)), serial)
            P.chans.append(c)
            return c

        A = Alloc(sb)
        o_xT = A.take(8 * TOK * 4)
        xT = A.view(o_xT, 8 * TOK, F32).rearrange("p (k t) -> p k t", k=8)
        o_hT = A.take(8 * TOK * 2)
        hT = A.view(o_hT, 8 * TOK, BF16).rearrange("p (k t) -> p k t", k=8)
        o_c = A.take(NCONST * 128 * 4)
        cst = A.view(o_c, NCONST * 128, F32)
        identF = cst[:, 0:128]
        onesF = cst[:, 7 * 128:8 * 128]
        o_cb = A.take(NCONST * 128 * 2)
        cstb = A.view(o_cb, NCONST * 128, BF16)
        identB, triFb, triBb, onesB = (cstb[:, i * 128:(i + 1) * 128] for i in (0, 2, 4, 7))
        o_cols = A.take(240 * 4)
        cols = A.view(o_cols, 240, F32)
        o_mod = A.take(144 * 4)
        modT = A.view(o_mod, 144, F32).rearrange("p (j v) -> p j v", v=2)
        o_scl = A.take(3 * 16 * 4)
        sclT = A.view(o_scl, 48, F32).rearrange("p (i k v) -> p i k v", i=3, k=8)
        o_gat = A.take(3 * 16 * 4)
        gatT = A.view(o_gat, 48, F32).rearrange("p (i k v) -> p i k v", i=3, k=8)
        o_sc = A.take(16 * 2)
        scT = A.view(o_sc, 16, BF16)
        SLOT = 8192
        NSLOT = 4
        slots = []
        for i in range(NSLOT):
            o = A.take(SLOT)
            slots.append((o, chan("slot%d" % i)))
        o_rstd = [A.take(2048) for _ in range(2)]
        o_ln = A.take(2048)
        o_tmp = [A.take(2048) for _ in range(3)]
        o_stage = [A.take(4096) for _ in range(2)]
        stage_ch = [chan("stage%d" % i) for i in range(2)]
        arena0 = A.take(0)
        ARENA = SB_BYTES - arena0

        def arena(off, nelem, dt):
            assert off + nelem * _esize(dt) <= ARENA, (off, nelem)
            return A.view(arena0 + off, nelem, dt)

        bank_i = [0]
        bank_mod = [8]

        def bank():
            b = bank_i[0] % bank_mod[0]
            bank_i[0] += 1
            return ps[:, b * 512:(b + 1) * 512]

        slot_i = [0]

        def next_slot():
            s = slots[slot_i[0] % NSLOT]
            slot_i[0] += 1
            return s

        rr = {"tmp": 0, "rstd": 0, "ev": 0}

        def tmpbuf(dt=F32):
            o = o_tmp[rr["tmp"] % 3]
            rr["tmp"] += 1
            return A.view(o, 512, dt)

        cg = chan("constgrp", serial=False)
        P.dma("sp", cst, consts_d, cg)
        rtA = arena(0, 128, F32)
        rtB = arena(512, 128, F32)
        P.dma("sp", rtA[0:120, :], rt[0:120, :], cg)
        P.dma("sp", rtB[0:120, :], rt[120:240, :], cg)
        P.copy(cstb, cst)
        bk = bank()
        P.transpose(bk[:, 0:120], rtA[0:120, :], identF[0:120, 0:120])
        P.transpose(bk[:, 128:248], rtB[0:120, :], identF[0:120, 0:120])
        P.copy(cols[:, 0:120], bk[:, 0:120])
        P.copy(cols[:, 120:240], bk[:, 128:248])
        colsB = cols[:, 120:240]
        P.act(scT, cols[:, 72:88], AF.Silu)
        def mod_block(hb):
            so, sch = next_slot()
            sv = A.view(so, 8 * 512, BF16).rearrange("p (k c) -> p k c", k=8)
            P.dma("pool", sv, w_mod[:, hb * 512:(hb + 1) * 512].rearrange("(k p) c -> p k c", p=128), sch)
            bk = bank()
            for mm_ in range(4):
                for k in range(8):
                    P.mm(bk[:, 2 * mm_:2 * mm_ + 2], sv[:, k, mm_ * 128:(mm_ + 1) * 128],
                         scT[:, k:16:8], start=(k == 0), stop=(k == 7))
            P.tt(modT[:, hb * 4:hb * 4 + 4, :], bk[:, 0:8].rearrange("p (j v) -> p j v", v=2),
                 cols[:, hb * 4:hb * 4 + 4].unsqueeze(2).to_broadcast([128, 4, 2]), ALU.add)

        def mod_scl(i):
            nw = cols[:, 88 + 8 * i:96 + 8 * i].unsqueeze(2).to_broadcast([128, 8, 2])
            P.stt(sclT[:, i], modT[:, (3 * i + 1) * 8:(3 * i + 2) * 8, :], 1.0, nw, ALU.add, ALU.mult)

        def mod_gate(i):
            P.ts(gatT[:, i], modT[:, (3 * i + 2) * 8:(3 * i + 3) * 8, :], 0.5 if i != 1 else 1.0, None, ALU.mult)

        def mod_derived(i):
            mod_scl(i)
            mod_gate(i)

        def mod_next():
            hb_ = mod_rest.pop(0)
            mod_block(hb_)
            if hb_ == 5:
                mod_gate(0)

        xst = [arena(2048 + c * 4096, 1024, F32) for c in range(NCH)]
        xch = [chan("xld%d" % c) for c in range(NCH)]
        for c in range(NCH):
            P.dma("sp", xst[c], xin[c * 128:(c + 1) * 128, :], xch[c])
        for c in range(NCH):
            st = xst[c]
            for half in range(2):
                bk = bank()
                for kk in range(4):
                    k = half * 4 + kk
                    P.transpose(bk[:, kk * 128:(kk + 1) * 128], st[:, k * 128:(k + 1) * 128], identF)
                P.copy(xT[:, half * 4:half * 4 + 4, c * 128:(c + 1) * 128],
                       bk.rearrange("p (k t) -> p k t", k=4), eng=("dve" if half == 0 else "act"))

        for hb in range(4):
            mod_block(hb)
        mod_scl(0)
        mod_rest = list(range(4, 18))

        def rstd_tile(t, sq_off):
            if sq_off is None:
                assert o_stage[1] == o_stage[0] + 4096
                sq = A.view(o_stage[0], 8 * 512, BF16).rearrange("p (k t) -> p k t", k=8)
            else:
                sq = arena(sq_off, 8 * 512, BF16).rearrange("p (k t) -> p k t", k=8)
            for k in range(8):
                P.act(sq[:, k], xT[:, k, t * 512:(t + 1) * 512], AF.Square)
            bk = bank()
            for k in range(8):
                P.mm(bk, onesB, sq[:, k], start=(k == 0), stop=(k == 7))
            lnv = A.view(o_ln, 512, F32)
            P.act(lnv, bk, AF.Ln, bias=EPS, scale=1.0 / D)
            rs = A.view(o_rstd[rr["rstd"] % 2], 512, F32)
            rr["rstd"] += 1
            P.act(rs, lnv, AF.Exp, scale=-0.5)
            return rs

        def norm_mod(i, sq_off):
            for t in range(3):
                v = 0 if t == 0 else 1
                rs = rstd_tile(t, sq_off)
                for k in range(8):
                    tm = tmpbuf()
                    P.tt(tm, xT[:, k, t * 512:(t + 1) * 512], rs, ALU.mult)
                    if k % 2 == 0:
                        P.act(hT[:, k, t * 512:(t + 1) * 512], tm, AF.Identity,
                              bias=modT[:, 3 * i * 8 + k, v:v + 1], scale=sclT[:, i, k, v:v + 1])
                    else:
                        P.ts(hT[:, k, t * 512:(t + 1) * 512], tm, sclT[:, i, k, v:v + 1],
                             modT[:, 3 * i * 8 + k, v:v + 1], ALU.mult, ALU.add)

        def ffn(i, wi, wo):
            actT = arena(0, 22 * TOK, BF16).rearrange("p (j t) -> p j t", j=22)
            for b in range(11):
                so, sch = next_slot()
                sv = A.view(so, 8 * 512, BF16).rearrange("p (k c) -> p k c", k=8)
                P.dma("pool", sv[:, :, 0:256],
                      wi[:, b * 256:(b + 1) * 256].rearrange("(k p) c -> p k c", p=128), sch)
                P.dma("pool", sv[:, :, 256:512],
                      wi[:, DFF + b * 256:DFF + (b + 1) * 256].rearrange("(k p) c -> p k c", p=128), sch)
                for jj in range(2):
                    j = 2 * b + jj
                    for t in range(3):
                        pg = bank()
                        pu = bank()
                        for k in range(8):
                            P.mm(pg, sv[:, k, jj * 128:(jj + 1) * 128], hT[:, k, t * 512:(t + 1) * 512],
                                 start=(k == 0), stop=(k == 7))
                        for k in range(8):
                            P.mm(pu, sv[:, k, 256 + jj * 128:256 + (jj + 1) * 128],
                                 hT[:, k, t * 512:(t + 1) * 512], start=(k == 0), stop=(k == 7))
                        sgt = tmpbuf()
                        P.act(sgt, pg, AF.Silu)
                        P.tt(actT[:, j, t * 512:(t + 1) * 512], sgt, pu, ALU.mult)
                if i == 0 and mod_rest:
                    mod_next()
            for b in range(8):
                so, sch = next_slot()
                sv = A.view(so, 22 * 128, BF16).rearrange("p (k c) -> p k c", k=22)
                P.dma("pool", sv, wo[:, b * 128:(b + 1) * 128].rearrange("(k p) c -> p k c", p=128), sch)
                if i == 0 and mod_rest:
                    mod_next()
                for mm_ in range(1):
                    m = b
                    for t in range(3):
                        v = 0 if t == 0 else 1
                        bk = bank()
                        for kk in range(22):
                            P.mm(bk, sv[:, kk, mm_ * 128:(mm_ + 1) * 128], actT[:, kk, t * 512:(t + 1) * 512],
                                 start=(kk == 0), stop=(kk == 21))
                        xs = xT[:, m, t * 512:(t + 1) * 512]
                        P.stt(xs, bk, gatT[:, i, m, v:v + 1], xs, ALU.mult, ALU.add)

        SQ_OFF = None
        norm_mod(0, SQ_OFF)
        ffn(0, f1_in, f1_out)
        while mod_rest:
            mod_next()
        mod_derived(1)
        mod_derived(2)
        if stage >= 2:
            norm_mod(1, SQ_OFF)
            bank_mod[0] = 8
            D1TRI = [cst[:, 128:384], cst[:, 384:640]]
            TRI = [cst[:, 256:384], cst[:, 512:640]]
            D2 = [cst[:, 640:768], cst[:, 768:896]]
            D1TRIb = [cstb[:, 128:384], cstb[:, 384:640]]
            TRIb = [cstb[:, 256:384], cstb[:, 512:640]]
            D2b = [cstb[:, 640:768], cstb[:, 768:896]]

            class Rot:
                def __init__(self, items):
                    self.items, self.i = items, 0

                def next(self):
                    x_ = self.items[self.i % len(self.items)]
                    self.i += 1
                    return x_

            PSB = [ps[:, i_ * 512:(i_ + 1) * 512] for i_ in range(8)]
            MASKB = [triFb, triBb]
            LASTCOL = [255, 128]
            YINF = ps[:, 6 * 512:7 * 512]
            YINB = ps[:, 7 * 512:8 * 512]

            class Bump:
                def __init__(self, base, size):
                    self.base, self.size, self.off = base, size, 0

                def take(self, nelem, dt, align=512):
                    self.off = (self.off + align - 1) // align * align
                    o = self.base + self.off
                    self.off += nelem * _esize(dt)
                    assert self.off <= self.size, (self.off, self.size)
                    return A.view(o, nelem, dt)

            MA = Bump(arena0, ARENA)
            MM = Bump(o_rstd[0], 20480)
            och = chan("outstate")
            sch_ld = [chan("stld%d" % i) for i in range(2)]
            wch = [chan("wa2ch"), chan("bach")]
            bcT = MA.take(1360, F32)
            negm = [MA.take(512, BF16).rearrange("p (h t) -> p h t", h=4) for _ in range(2)]
            rowtab = MM.take(1360, F32)
            cg2 = chan("constgrp2", serial=False)
            for (lo, src) in ((0, gnw), (256, snw), (1280, dsk), (1296, dtb), (1328, alog)):
                n = src.shape[1]
                P.dma("sp", rowtab[0:1, lo:lo + n], src, cg2)
            for (lo, hi) in ((0, 512), (512, 1024), (1024, 1360)):
                bk = bank()
                P.mm(bk[:, 0:hi - lo], onesF[0:1, :], rowtab[0:1, lo:hi])
                P.copy(bcT[:, lo:hi], bk[:, 0:hi - lo])
            P.act(bcT[:, 1328:1360], bcT[:, 1328:1360], AF.Exp)
            P.ts(bcT[:, 1328:1360], bcT[:, 1328:1360], -1.0, None, ALU.mult)
            gnw_bc = bcT[:, 0:256]
            snw_bc = bcT[:, 256:1280]
            dsk_bc = bcT[:, 1280:1296]
            dtb_bc = bcT[:, 1296:1328]
            a_bc = bcT[:, 1328:1360]
            for d in range(2):
                P.ts(negm[d], TRI[d].unsqueeze(1).to_broadcast([128, 4, 128]), -1.0, 100000.0, ALU.add, ALU.mult)
            MA_base = MA.off
            MM.off = 0

            def load_slot(pieces):
                so, sch = next_slot()
                tot = sum(p.shape[1] for p in pieces)
                assert 8 * tot * 2 <= SLOT
                sv = A.view(so, 8 * tot, BF16).rearrange("p (k c) -> p k c", k=8)
                o = 0
                for p in pieces:
                    n = p.shape[1]
                    P.dma("pool", sv[:, :, o:o + n], p.rearrange("(k p) c -> p k c", p=128), sch)
                    o += n
                return sv

            def mixer_sg(T0, NT, seqs, v, is2d):
                NCK = NT // 128
                NTL = NT // 512
                MA.off = MA_base
                MM.off = 0
                dt = MA.take(NCK * 32, F32).rearrange("p (c h) -> p c h", h=32)
                dta = MA.take(NCK * 32, F32).rearrange("p (c h) -> p c h", h=32)
                dtah = MA.take(NCK * 32, BF16, align=256).rearrange("p (c h) -> p c h", h=32)
                dtal = MA.take(NCK * 32, BF16, align=256).rearrange("p (c h) -> p c h", h=32)
                MA_ssd0 = MA.off
                afT = [MA.take(NT, BF16), MA.take(NT, BF16)]
                wa2 = MA.take(1024, BF16)
                ba = MA.take(1024, BF16)
                P.dma("pool", wa2[0:16, :].rearrange("k (d c) -> k d c", d=2), w_a2.rearrange("d k c -> k d c"), wch[0])
                P.dma("pool", ba[0:1, :], b_a.rearrange("(o d) c -> o (d c)", o=1), wch[1])
                sv = load_slot([w_in[:, 3072:3104], w_in[:, 5664:5696]])
                for d in range(2):
                    for tl in range(NTL):
                        bk = bank()
                        for k in range(8):
                            P.mm(bk[0:16, :], sv[:, k, 16 * d:16 * d + 16], hT[:, k, T0 + tl * 512:T0 + (tl + 1) * 512],
                                 start=(k == 0), stop=(k == 7))
                        P.copy(afT[d][0:16, tl * 512:(tl + 1) * 512], bk[0:16, :])
                bk = bank()
                for c in range(NCK):
                    for k in range(8):
                        P.mm(bk[:, c * 32:(c + 1) * 32], hT[:, k, T0 + c * 128:T0 + (c + 1) * 128], sv[:, k, 32:64],
                             start=(k == 0), stop=(k == 7))
                P.tt(dt, bk[:, 0:NCK * 32].rearrange("p (c h) -> p c h", h=32),
                     dtb_bc.unsqueeze(1).to_broadcast([128, NCK, 32]), ALU.add)
                P.act(dt, dt, AF.Exp)
                P.act(dt, dt, AF.Ln, bias=1.0)
                P.tt(dta, dt, a_bc.unsqueeze(1).to_broadcast([128, NCK, 32]), ALU.mult)
                P.copy(dtah, dta)
                P.tt(dtal, dta, dtah, ALU.subtract)

                MA_sg = MA.off

                def outproj(kT_list, rows0, nk):
                    ncol = 512 if nk > 2 else 1024
                    for blk in range(1024 // ncol):
                        so, sch = next_slot()
                        svo = A.view(so, nk * ncol, BF16).rearrange("p (k c) -> p k c", k=nk)
                        P.dma("pool", svo, w_out[rows0:rows0 + nk * 128, blk * ncol:(blk + 1) * ncol]
                              .rearrange("(k p) c -> p k c", p=128), sch)
                        for mm_ in range(ncol // 128):
                            m = blk * (ncol // 128) + mm_
                            for tl in range(NTL):
                                bk = bank()
                                for kk in range(nk):
                                    P.mm(bk, svo[:, kk, mm_ * 128:(mm_ + 1) * 128],
                                         kT_list[kk][:, tl * 512:(tl + 1) * 512], start=(kk == 0), stop=(kk == nk - 1))
                                xs = xT[:, m, T0 + tl * 512:T0 + (tl + 1) * 512]
                                P.stt(xs, bk, gatT[:, 1, m, v:v + 1], xs, ALU.mult, ALU.add)

                NH = 4 if FLAGS["gla"] else 0
                MA.off = MA_sg
                MM.off = 0
                R1 = []
                R2 = []
                for par_ in range(2):
                    MA.off = (MA.off + 511) // 512 * 512
                    o_r1 = MA.base + MA.off
                    qT_ = MA.take(NT, BF16)
                    kT_ = MA.take(NT, BF16)
                    ktok_ = MA.take(NCK * 128, BF16).rearrange("p (c f) -> p c f", f=128)
                    R1.append((qT_, kT_, ktok_, A.view(o_r1, 2 * NT, BF16).rearrange("p (j t) -> p j t", j=2)))
                    vtok_ = MA.take(NCK * 256, BF16).rearrange("p (c f) -> p c f", f=256)
                    rs_ = MA.take(NCK * 256, BF16).rearrange("p (c f) -> p c f", f=256)
                    R2.append((vtok_, rs_))
                o_l = [None, None]
                lh = []
                ll = []
                for d in range(2):
                    MA.off = (MA.off + 511) // 512 * 512
                    o_l[d] = MA.base + MA.off
                    lh.append(MA.take(NCK * 128, BF16, align=2).rearrange("p (c f) -> p c f", f=128))
                    ll.append(MA.take(NCK * 128, BF16, align=2).rearrange("p (c f) -> p c f", f=128))
                c3 = lambda: MA.take(NCK * 128, BF16).rearrange("p (c f) -> p c f", f=128)
                qi = [c3(), c3()]
                ki = [c3(), c3()]
                qg = [c3(), c3()]
                kst = [c3(), c3()]
                comb = c3()
                Sstore = [A.view(o_l[d], NCK * 256, BF16).rearrange("p (c f) -> p c f", f=256) for d in range(2)]
                dec = MM.take(2 * NCK, F32, align=256).rearrange("p (d c) -> p d c", d=2)
                o_ph2 = MM.off
                T_e = [MM.take(512, F32) for _ in range(2)]
                T_l = [MM.take(512, F32).rearrange("p (c f) -> p c f", f=128) for _ in range(2)]
                T_Eqg = [MM.take(512, F32).rearrange("p (c f) -> p c f", f=256) for _ in range(2)]
                T_Ek = [MM.take(256, F32).rearrange("p (c f) -> p c f", f=128) for _ in range(2)]
                T_kd = [MM.take(512, F32).rearrange("p (c f) -> p c f", f=128) for _ in range(2)]
                MM.off = o_ph2
                T_cF = MM.take(512, F32).rearrange("p (c f) -> p c f", f=128)
                T_cB = MM.take(512, F32).rearrange("p (c f) -> p c f", f=128)
                Sst = [[MM.take(256, F32) for _ in range(2)] for _ in range(len(seqs))]
                T_on = [MM.take(512, F32).rearrange("p (c f) -> p c f", f=256) for _ in range(2)]
                T_og = [MM.take(512, BF16).rearrange("p (c f) -> p c f", f=256) for _ in range(2)]
                T_junk = MM.take(256, F32)
                T_sq = [MM.take(8, F32, align=512) for _ in range(2)]
                HB = Rot([PSB[5], PSB[6], PSB[7]])
                MB = Rot([PSB[0], PSB[1], PSB[2], PSB[3]])
                JUNK = PSB[4]
                NFILL = FLAGS.get("nfill", 0)

                def filler(n=1):
                    for _ in range(n):
                        P.mm(JUNK.rearrange("p (h t) -> p h t", h=4), identB, negm[0])


                T_th = [MM.take(256, F32) for _ in range(2)]
                thc = [0]

                def gen_proj_vr(h):
                    vtok, rs = R2[h % 2]
                    sB = load_slot([w_in[:, 1024 + h * 256:1024 + (h + 1) * 256],
                                    w_in[:, 2048 + h * 256:2048 + (h + 1) * 256]])
                    for c in range(NCK):
                        bk = HB.next()
                        for k in range(8):
                            P.mm(bk, hT[:, k, T0 + c * 128:T0 + (c + 1) * 128], sB[:, k, :],
                                 start=(k == 0), stop=(k == 7))
                        P.copy(vtok[:, c, :], bk[:, 0:256])
                        th = T_th[thc[0] % 2]
                        thc[0] += 1
                        P.act(th, bk[:, 256:512], AF.Tanh, scale=0.5)
                        P.ts(th, th, 1.0, 0.5, ALU.add, ALU.mult)
                        P.tt(rs[:, c, :], th, bk[:, 256:512], ALU.mult)
                        yield

                def gen_proj(h):
                    qT, kT, ktok, _ = R1[h % 2]
                    sA = load_slot([w_in[:, h * 128:(h + 1) * 128], w_in[:, 512 + h * 128:512 + (h + 1) * 128]])
                    for tl in range(NTL):
                        for (dst, c0, sc_) in ((qT, 0, 128.0 ** -0.5), (kT, 128, 1.0)):
                            bk = HB.next()
                            for k in range(8):
                                P.mm(bk, sA[:, k, c0:c0 + 128], hT[:, k, T0 + tl * 512:T0 + (tl + 1) * 512],
                                     start=(k == 0), stop=(k == 7))
                            P.act(dst[:, tl * 512:(tl + 1) * 512], bk, AF.Copy, scale=sc_)
                            yield
                    for c4 in range(NCK // 4):
                        bk = HB.next()
                        for cc in range(4):
                            c = c4 * 4 + cc
                            for k in range(8):
                                P.mm(bk[:, cc * 128:(cc + 1) * 128], hT[:, k, T0 + c * 128:T0 + (c + 1) * 128],
                                     sA[:, k, 128:256], start=(k == 0), stop=(k == 7))
                        P.copy(ktok[:, c4 * 4:c4 * 4 + 4, :], bk.rearrange("p (c f) -> p c f", f=128))
                        yield

                def gen_outproj(h):
                    oT = R1[h % 2][3]
                    so, sch = next_slot()
                    svo = A.view(so, 2 * 1024, BF16).rearrange("p (k c) -> p k c", k=2)
                    P.dma("pool", svo, w_out[h * 256:(h + 1) * 256, :].rearrange("(k p) c -> p k c", p=128), sch)
                    for m in range(8):
                        for tl in range(NTL):
                            bk = HB.next()
                            for kk in range(2):
                                P.mm(bk, svo[:, kk, m * 128:(m + 1) * 128], oT[:, kk, tl * 512:(tl + 1) * 512],
                                     start=(kk == 0), stop=(kk == 1))
                            xs = xT[:, m, T0 + tl * 512:T0 + (tl + 1) * 512]
                            P.stt(xs, bk, gatT[:, 1, m, v:v + 1], xs, ALU.mult, ALU.add)
                            yield

                def gen_main(h):
                    qT, kT, ktok, oT = R1[h % 2]
                    vtok, rs = R2[h % 2]
                    qT3 = qT.rearrange("p (c t) -> p c t", t=128)
                    kT3 = kT.rearrange("p (c t) -> p c t", t=128)
                    for d in range(2):
                        for c4 in range(NCK // 4):
                            bk = MB.next()
                            for cc in range(4):
                                c = c4 * 4 + cc
                                P.mm(bk[:, cc * 128:(cc + 1) * 128], afT[d][0:16, c * 128:(c + 1) * 128],
                                     wa2[0:16, d * 512 + h * 128:d * 512 + (h + 1) * 128], start=True, stop=False)
                                P.mm(bk[:, cc * 128:(cc + 1) * 128], onesB[0:1, :],
                                     ba[0:1, d * 512 + h * 128:d * 512 + (h + 1) * 128], start=False, stop=True)
                            e = T_e[(d * (NCK // 4) + c4) % 2]
                            lf = T_l[(d * (NCK // 4) + c4) % 2]
                            P.act(e, bk, AF.Exp, scale=-1.0)
                            P.act(lf, e.rearrange("p (c f) -> p c f", f=128), AF.Ln, bias=1.0)
                            P.copy(lh[d][:, c4 * 4:c4 * 4 + 4, :], lf)
                            P.tt(ll[d][:, c4 * 4:c4 * 4 + 4, :], lf, lh[d][:, c4 * 4:c4 * 4 + 4, :], ALU.subtract)
                            filler(NFILL)
                            yield "ln"
                    it = 0
                    for d in range(2):
                        for c2 in range(NCK // 2):
                            bk = MB.next()
                            bk3 = bk.rearrange("p (c f) -> p c f", f=256)
                            for cc in range(2):
                                P.mm(bk3[:, cc, :], lh[d][:, c2 * 2 + cc, :], D1TRIb[d], start=True, stop=False)
                                P.mm(bk3[:, cc, :], ll[d][:, c2 * 2 + cc, :], D1TRIb[d], start=False, stop=True)
                            Eqg = T_Eqg[it % 2]
                            Ek = T_Ek[it % 2]
                            it += 1
                            P.act(Eqg, bk3, AF.Exp, scale=-1.0 / 16)
                            P.act(Ek, bk3[:, :, 0:128], AF.Exp, scale=1.0 / 16)
                            cs_ = slice(c2 * 2, c2 * 2 + 2)
                            P.tt(qi[d][:, cs_, :], qT3[:, cs_, :], Eqg[:, :, 0:128], ALU.mult)
                            P.tt(ki[d][:, cs_, :], kT3[:, cs_, :], Ek, ALU.mult)
                            P.tt(qg[d][:, cs_, :], qT3[:, cs_, :], Eqg[:, :, 128:256], ALU.mult)
                            P.copy(dec[:, d, cs_], Eqg[:, :, LASTCOL[d]], eng="act")
                            filler(NFILL)
                            yield "exp"
                    it = 0
                    for d in range(2):
                        for c4 in range(NCK // 4):
                            bk = MB.next()
                            P.mm(bk.rearrange("p (c f) -> p c f", f=128), D2b[d], lh[d][:, c4 * 4:c4 * 4 + 4, :],
                                 start=True, stop=False)
                            P.mm(bk.rearrange("p (c f) -> p c f", f=128), D2b[d], ll[d][:, c4 * 4:c4 * 4 + 4, :],
                                 start=False, stop=True)
                            kd = T_kd[it % 2]
                            it += 1
                            P.act(kd, bk.rearrange("p (c f) -> p c f", f=128), AF.Exp, scale=-1.0 / 16)
                            P.tt(kst[d][:, c4 * 4:c4 * 4 + 4, :], ktok[:, c4 * 4:c4 * 4 + 4, :], kd, ALU.mult)
                            filler(NFILL)
                            yield "exp"
                    for c4 in range(NCK // 4):
                        bF = MB.next()
                        bB = MB.next()
                        for cc in range(4):
                            c = c4 * 4 + cc
                            P.mm(bF[:, cc * 128:(cc + 1) * 128], ki[0][:, c, :], qi[0][:, c, :])
                            P.mm(bB[:, cc * 128:(cc + 1) * 128], ki[1][:, c, :], qi[1][:, c, :])
                        P.tt(T_cF, bF.rearrange("p (c f) -> p c f", f=128),
                             MASKB[0].unsqueeze(1).to_broadcast([128, 4, 128]), ALU.mult)
                        P.tt(T_cB, bB.rearrange("p (c f) -> p c f", f=128),
                             MASKB[1].unsqueeze(1).to_broadcast([128, 4, 128]), ALU.mult)
                        P.tt(comb[:, c4 * 4:c4 * 4 + 4, :], T_cF, T_cB, ALU.add)
                        filler(NFILL)
                        yield "exp"
                    for si, (cs, init, oidx) in enumerate(seqs):
                        for d in range(2):
                            if init == "zero":
                                P.memset(Sst[si][d], 0.0, eng="dve")
                            else:
                                P.dma("sp", Sst[si][d], sg_in[d, h], sch_ld[d])
                    nstep = max(len(cs) for cs, _, _ in seqs)
                    for i in range(nstep):
                        for si, (cs, init, oidx) in enumerate(seqs):
                            for d in range(2):
                                c = cs[i] if d == 0 else cs[-1 - i]
                                S = Sst[si][d]
                                P.copy(Sstore[d][:, c, :], S, eng="act")
                                if init != "zero" and i == len(cs) - 1:
                                    continue
                                bk = MB.next()
                                P.mm(bk[:, 0:256], kst[d][:, c, :], vtok[:, c, :])
                                P.stt(S, S, dec[:, d, c:c + 1], bk[:, 0:256], ALU.mult, ALU.add)
                        filler(NFILL)
                        yield "exp"
                    for si, (cs, init, oidx) in enumerate(seqs):
                        if oidx is not None:
                            for d in range(2):
                                P.dma("sp", nsg_out[oidx, d, h], Sst[si][d], och)
                    oT4 = oT.rearrange("p j (c t) -> p j c t", t=128)
                    PO = Rot([PSB[0], PSB[1]])
                    BT_ = Rot([PSB[2], PSB[3]])
                    pos = {}

                    def g5_A(c2):
                        po = PO.next()
                        po3 = po.rearrange("p (c f) -> p c f", f=256)
                        pos[c2] = po3
                        for cc in range(2):
                            c = c2 * 2 + cc
                            P.mm(po3[:, cc, :], comb[:, c, :], vtok[:, c, :], start=True, stop=False)
                            P.mm(po3[:, cc, :], qg[0][:, c, :], Sstore[0][:, c, :], start=False, stop=False)
                            P.mm(po3[:, cc, :], qg[1][:, c, :], Sstore[1][:, c, :], start=False, stop=True)

                    def g5_B(c2):
                        par = c2 % 2
                        po3 = pos[c2]
                        sq = T_sq[par]
                        for cc in range(2):
                            P.act(T_junk, po3[:, cc, :], AF.Square, accum_out=sq[:, cc:cc + 1])
                        P.act(sq[:, 2:4], sq[:, 0:2], AF.Ln, bias=EPS, scale=1.0 / 256)
                        P.act(sq[:, 4:6], sq[:, 2:4], AF.Exp, scale=-0.5)
                        on = T_on[par]
                        for cc in range(2):
                            P.stt(on[:, cc, :], po3[:, cc, :], sq[:, 4 + cc:5 + cc], gnw_bc, ALU.mult, ALU.mult)
                        og = T_og[par]
                        P.tt(og, on, rs[:, c2 * 2:c2 * 2 + 2, :], ALU.mult)
                        bt = BT_.next().bitcast(BF16)
                        for j in range(2):
                            for cc in range(2):
                                P.transpose(bt[:, (j * 2 + cc) * 128:(j * 2 + cc + 1) * 128],
                                            og[:, cc, j * 128:(j + 1) * 128], identB)
                        P.copy(oT4[:, :, c2 * 2:c2 * 2 + 2, :],
                               bt[:, 0:512].rearrange("p (j c t) -> p j c t", j=2, c=2), eng="act")

                    n2 = NCK // 2
                    g5_A(0)
                    for c2 in range(n2):
                        if c2 + 1 < n2:
                            g5_A(c2 + 1)
                        g5_B(c2)
                        filler(NFILL)
                        yield "ln"

                def drain(gen):
                    for _ in gen:
                        pass

                def chain(*gens):
                    for g_ in gens:
                        for _ in g_:
                            yield

                def step(gen):
                    try:
                        next(gen)
                        return True
                    except StopIteration:
                        return False

                if NH:
                    drain(gen_proj(0))
                    drain(gen_proj_vr(0))
                for h in range(NH):
                    heavy = []
                    if h >= 1:
                        heavy.append(gen_outproj(h - 1))
                    if h + 1 < NH:
                        heavy.append(gen_proj(h + 1))
                    q_plain = chain(*heavy)
                    q_tanh = gen_proj_vr(h + 1) if h + 1 < NH else iter(())
                    a_plain = a_tanh = True
                    for tag in gen_main(h):
                        if tag == "exp" and a_tanh:
                            a_tanh = step(q_tanh)
                            if a_tanh:
                                continue
                        if a_plain:
                            a_plain = step(q_plain)
                    if a_plain:
                        drain(q_plain)
                    if a_tanh:
                        drain(q_tanh)
                if NH:
                    drain(gen_outproj(NH - 1))

                MA.off = MA_ssd0
                yzk = [MA.take(NCK * 512, BF16).rearrange("p (c f) -> p c f", f=512) for _ in range(2)]
                ssacc = MA.take(2 * NCK, F32, align=256)
                MA_ssd = MA.off
                PADN = 1188 if is2d else 516
                if not FLAGS["ssd"]:
                    return
                for g in range(2):
                    bank_mod[0] = 8
                    MA.off = MA_ssd
                    MM.off = 0
                    x_tok = MA.take(NCK * 512, BF16).rearrange("p (c f) -> p c f", f=512)
                    B_tok = MA.take(NCK * 128, BF16).rearrange("p (c f) -> p c f", f=128)
                    xcBC = MA.take(2 * NT, BF16).rearrange("p (j t) -> p j t", j=2)
                    MA_scan = MA.off
                    xcX = MA.take(4 * NT, BF16).rearrange("p (j t) -> p j t", j=4)
                    padv = MA.take(6 * PADN, BF16).rearrange("p (j t) -> p j t", j=6)
                    Sst = [[MM.take(512, F32) for _ in range(2)] for _ in range(len(seqs))]
                    SM = MM.take(2 * 4 * NCK * 8, F32).rearrange("p (d k c h) -> p d k c h", d=2, k=4, c=NCK)
                    MM_ph = MM.off
                    diag2 = [[MM.take(128, BF16, align=256) for _ in range(9)] for _ in range(2)]
                    P.memset(padv, 0.0, eng="dve")
                    sX = load_slot([w_in[:, 4128 + g * 512:4128 + (g + 1) * 512]])
                    sBC = load_slot([w_in[:, 5152 + g * 128:5152 + (g + 1) * 128],
                                     w_in[:, 5408 + g * 128:5408 + (g + 1) * 128]])

                    def xc(j):
                        return xcX[:, j, :] if j < 4 else xcBC[:, j - 4, :]

                    def padint(j, tl, sh=(0, 0)):
                        if is2d:
                            pv = padv[:, j, :].rearrange("p (r w) -> p r w", r=18)
                            return pv[:, 8 * tl + 1 + sh[0]:8 * tl + 9 + sh[0], 1 + sh[1]:65 + sh[1]]
                        pv = padv[:, j, :].rearrange("p (s w) -> p s w", s=2)
                        return pv[:, :, 1 + sh[1]:257 + sh[1]]

                    def as3(ap2):
                        return ap2.rearrange("p (r w) -> p r w", r=(8 if is2d else 2))

                    for j in range(6):
                        for tl in range(NTL):
                            bk = bank()
                            for k in range(8):
                                lw_ = sX[:, k, j * 128:(j + 1) * 128] if j < 4 else sBC[:, k, (j - 4) * 128:(j - 3) * 128]
                                P.mm(bk, lw_, hT[:, k, T0 + tl * 512:T0 + (tl + 1) * 512], start=(k == 0), stop=(k == 7))
                            P.copy(padint(j, tl), as3(bk), eng="act")
                    taps = [(ky, kx) for ky in range(3) for kx in range(3)] if is2d else [(1, kx) for kx in range(3)]
                    for j in range(6):
                        ci = (g * 4 + j) if j < 4 else (8 + g if j == 4 else 10 + g)
                        diag = diag2[j % 2]
                        for ti, (ky, kx) in enumerate(taps):
                            tap = ky * 3 + kx
                            P.ts(diag[ti], identB, colsB[:, tap * 12 + ci:tap * 12 + ci + 1], None, ALU.mult)
                        for tl in range(NTL):
                            bk = bank()
                            for ti, (ky, kx) in enumerate(taps):
                                P.mm(as3(bk), diag[ti], padint(j, tl, (ky - 1, kx - 1)),
                                     start=(ti == 0), stop=(ti == len(taps) - 1))
                            P.act(xc(j)[:, tl * 512:(tl + 1) * 512], bk, AF.Silu, bias=colsB[:, 108 + ci:109 + ci])
                    for c in range(NCK):
                        bt = bank().bitcast(BF16)
                        for j in range(4):
                            P.transpose(bt[:, j * 128:(j + 1) * 128], xcX[:, j, c * 128:(c + 1) * 128], identB)
                        P.copy(x_tok[:, c, :], bt[:, 0:512], eng=("act" if c % 2 else "dve"))
                    for c4 in range(NCK // 4):
                        bt = bank().bitcast(BF16)
                        for cc in range(4):
                            c = c4 * 4 + cc
                            P.transpose(bt[:, cc * 128:(cc + 1) * 128], xcBC[:, 0, c * 128:(c + 1) * 128], identB)
                        P.copy(B_tok[:, c4 * 4:c4 * 4 + 4, :], bt[:, 0:512].rearrange("p (c f) -> p c f", f=128))
                    BT = xcBC[:, 0, :]
                    CT = xcBC[:, 1, :]
                    sZ = load_slot([w_in[:, 3104 + g * 512:3104 + (g + 1) * 512]])
                    MA.off = MA_scan
                    Sstore = [MA.take(NCK * 512, BF16).rearrange("p (c f) -> p c f", f=512) for _ in range(2)]
                    CB_all = MA.take(NCK * 128, F32).rearrange("p (c f) -> p c f", f=128)
                    T_t1 = [MA.take(512, F32) for _ in range(2)]
                    T_t2 = MA.take(512, F32)
                    T_zs = MA.take(512, F32)
                    MM.off = MM_ph
                    T_expL = [MM.take(512, F32).rearrange("p (h t) -> p h t", h=4) for _ in range(2)]
                    T_scr = [MM.take(512, BF16).rearrange("p (h t) -> p h t", h=4) for _ in range(4)]
                    T_xw = [MM.take(512, BF16) for _ in range(2)]
                    T_dsk = [(MM if is2d else MA).take(128, BF16, align=256) for _ in range(8)]
                    T_t3 = MM.take(512, F32) if is2d else MA.take(512, F32)
                    v3 = lambda a: a.rearrange("p (h q) -> p h q", h=8)
                    bc3 = lambda a: a.unsqueeze(2).to_broadcast([128, 8, 64])
                    n8 = NCK * 8
                    for d in range(2):
                        cl = slice(d * 16 + 8 * g, d * 16 + 8 * g + 8)
                        bk = bank()
                        o3 = lambda k_: bk[:, k_ * n8:(k_ + 1) * n8].rearrange("p (c h) -> p c h", h=8)
                        for k_, L_ in enumerate((TRIb[d], D2b[d], onesB)):
                            P.mm(o3(k_), L_, dtah[:, :, cl], start=True, stop=False)
                            P.mm(o3(k_), L_, dtal[:, :, cl], start=False, stop=True)
                        P.ts(SM[:, d, 0], dt[:, :, cl], 1e-18, None, ALU.max)
                        P.act(SM[:, d, 0], SM[:, d, 0], AF.Ln)
                        P.stt(SM[:, d, 0], o3(0), -1.0, SM[:, d, 0], ALU.mult, ALU.add)
                        P.act(SM[:, d, 1], o3(0), AF.Exp)
                        P.act(SM[:, d, 2], o3(1), AF.Exp)
                        P.tt(SM[:, d, 2], SM[:, d, 2], dt[:, :, cl], ALU.mult)
                        P.act(SM[:, d, 3], o3(2), AF.Exp)
                    for c4 in range(NCK // 4):
                        bk = bank()
                        for cc in range(4):
                            c = c4 * 4 + cc
                            P.mm(bk[:, cc * 128:(cc + 1) * 128], BT[:, c * 128:(c + 1) * 128], CT[:, c * 128:(c + 1) * 128])
                        P.copy(CB_all[:, c4 * 4:c4 * 4 + 4, :], bk.rearrange("p (c f) -> p c f", f=128), eng="act")
                    for si, (cs, init, oidx) in enumerate(seqs):
                        for d in range(2):
                            S = Sst[si][d]
                            if init == "zero":
                                P.memset(S, 0.0, eng="dve")
                            else:
                                stg = T_zs.rearrange("p (q n) -> p q n", q=4)
                                for q in range(4):
                                    P.dma("sp", stg[:, q, :],
                                          ss_in[d, 8 * g + 2 * q:8 * g + 2 * q + 2].rearrange("h p n -> (h p) n"),
                                          sch_ld[d])
                                bk = bank()
                                for q in range(4):
                                    P.transpose(bk[:, q * 128:(q + 1) * 128], stg[:, q, :], identF)
                                P.copy(S, bk)
                    nstep = max(len(cs) for cs, _, _ in seqs)
                    rxc = [0]
                    MISC = Rot([PSB[6], PSB[7]])

                    def rec_step(i):
                        for si, (cs, init, oidx) in enumerate(seqs):
                            for d in range(2):
                                c = cs[i] if d == 0 else cs[-1 - i]
                                S = Sst[si][d]
                                P.copy(Sstore[d][:, c, :], S, eng="act")
                                if init != "zero" and i == len(cs) - 1:
                                    continue
                                xw = T_xw[rxc[0] % 2]
                                rxc[0] += 1
                                P.tt(v3(xw), v3(x_tok[:, c, :]), bc3(SM[:, d, 2, c, :]), ALU.mult, eng="dve")
                                bk = MISC.next()
                                P.mm(bk, B_tok[:, c, :], xw)
                                P.tt(v3(S), v3(S), bc3(SM[:, d, 3, c, :]), ALU.mult, eng="dve")
                                P.tt(S, S, bk, ALU.add)

                    def rec_finish():
                        for si, (cs, init, oidx) in enumerate(seqs):
                            if oidx is not None:
                                for d in range(2):
                                    S = Sst[si][d]
                                    bk = MISC.next()
                                    for q in range(4):
                                        P.transpose(bk[:, q * 128:(q + 1) * 128], S[:, q * 128:(q + 1) * 128], identF)
                                    stg = T_zs.rearrange("p (q n) -> p q n", q=4)
                                    P.copy(stg, bk.rearrange("p (q n) -> p q n", q=4))
                                    for q in range(4):
                                        P.dma("sp", nss_out[oidx, d, 8 * g + 2 * q:8 * g + 2 * q + 2].rearrange("h p n -> (h p) n"),
                                              stg[:, q, :], och)

                    BBR = Rot([PSB[0], PSB[1], PSB[2], PSB[3]]) if not FLAGS.get("sfill", 0) else Rot([PSB[0], PSB[1], PSB[2]])
                    YIN1 = [PSB[4], PSB[5]]
                    rot = [0]
                    units = [(c, half) for c in range(NCK) for half in range(2)]
                    bbs = {}
                    for hh in range(8):
                        P.act(T_dsk[hh], identB, AF.Copy, scale=dsk_bc[:, 8 * g + hh:8 * g + hh + 1])

                    def st_A(u):
                        c, half = u
                        for d in range(2):
                            c0 = d * 16 + 8 * g + 4 * half
                            bb = BBR.next()
                            bbs[(u, d)] = bb
                            P.mm(bb.rearrange("p (h t) -> p h t", h=4), identB, negm[d], start=True, stop=False)
                            for h4 in range(4):
                                o_ = bb[:, h4 * 128:(h4 + 1) * 128]
                                P.mm(o_, dtah[:, c, c0 + h4:c0 + h4 + 1].to_broadcast([128, 128]), TRIb[d],
                                     start=False, stop=False)
                                P.mm(o_, dtal[:, c, c0 + h4:c0 + h4 + 1].to_broadcast([128, 128]), TRIb[d],
                                     start=False, stop=(h4 == 3))

                    ypend = []

                    def flush_y():
                        while ypend:
                            o_, s0_, s1_, dk_, xh_ = ypend.pop(0)
                            P.mm(o_, s0_, xh_, start=True, stop=False)
                            P.mm(o_, s1_, xh_, start=False, stop=False)
                            P.mm(o_, dk_, xh_, start=False, stop=True)

                    def st_B(u):
                        c, half = u
                        yin = YIN1[c % 2]
                        scs = []
                        for d in range(2):
                            bb = bbs[(u, d)]
                            eL = T_expL[rot[0] % 2]
                            sc_ = T_scr[rot[0] % 4]
                            rot[0] += 1
                            for h4 in range(4):
                                hh = half * 4 + h4
                                P.act(eL[:, h4, :], bb[:, h4 * 128:(h4 + 1) * 128], AF.Exp, bias=SM[:, d, 0, c, hh:hh + 1])
                            P.tt(sc_, eL, CB_all[:, c, :].unsqueeze(1).to_broadcast([128, 4, 128]), ALU.mult)
                            scs.append(sc_)
                            step_pend()
                        flush_y()
                        for h4 in range(4):
                            hh = half * 4 + h4
                            xh = x_tok[:, c, hh * 64:(hh + 1) * 64]
                            ypend.append((yin[:, hh * 64:(hh + 1) * 64], scs[0][:, h4, :], scs[1][:, h4, :], T_dsk[hh], xh))
                        step_pend()

                    def st_C(c):
                        par = c % 2
                        gc = slice(c * 128, (c + 1) * 128)
                        t1 = T_t1[par]
                        P.copy(t1, YIN1[par], eng="act")
                        yield
                        byf = MISC.next()
                        P.mm(byf, CT[:, gc], Sstore[0][:, c, :])
                        byb = MISC.next()
                        P.mm(byb, CT[:, gc], Sstore[1][:, c, :])
                        yield
                        P.tt(v3(T_t2), v3(byf), bc3(SM[:, 0, 1, c, :]), ALU.mult)
                        P.tt(v3(T_t3), v3(byb), bc3(SM[:, 1, 1, c, :]), ALU.mult)
                        yield
                        bz = MISC.next()
                        for k in range(8):
                            P.mm(bz, hT[:, k, T0 + c * 128:T0 + (c + 1) * 128], sZ[:, k, :], start=(k == 0), stop=(k == 7))
                        P.tt(t1, t1, T_t2, ALU.add, eng="dve")
                        P.tt(t1, t1, T_t3, ALU.add, eng="dve")
                        yield
                        P.act(T_zs, bz, AF.Tanh, scale=0.5)
                        yield
                        P.stt(T_zs, T_zs, 1.0, bz, ALU.add, ALU.mult)
                        yield
                        P.stt(yzk[g][:, c, :], T_zs, 0.5, t1, ALU.mult, ALU.mult)
                        P.act(T_t2, yzk[g][:, c, :], AF.Square, accum_out=ssacc[:, g * NCK + c:g * NCK + c + 1])
                        yield

                    pend = []

                    def step_pend():
                        if pend:
                            try:
                                next(pend[0])
                            except StopIteration:
                                pend.pop(0)
                                step_pend()

                    st_A(units[0])
                    defer_c = []
                    for ui, u in enumerate(units):
                        if ui + 1 < len(units):
                            st_A(units[ui + 1])
                        st_B(u)
                        spu = (nstep + 3) // 4
                        for i_ in range(ui * spu, min((ui + 1) * spu, nstep)):
                            rec_step(i_)
                            if i_ == nstep - 1:
                                rec_finish()
                        if u[1] == 1:
                            defer_c.append(u[0])
                        if ui >= 3 and ui % 2 == 1:
                            flush_y()
                            for c_ in defer_c:
                                gen_ = st_C(c_)
                                next(gen_)
                                pend.append(gen_)
                            defer_c = []
                    flush_y()
                    while pend:
                        step_pend()
                MA.off = MA_ssd
                MM.off = 0
                yT = MA.take(8 * NT, BF16).rearrange("p (j t) -> p j t", j=8)
                sst = MM.take(NCK, F32, align=256)
                rsd = MM.take(NCK, F32, align=256)
                T_yn = [MM.take(512, BF16) for _ in range(2)]
                P.tt(sst, ssacc[:, 0:NCK], ssacc[:, NCK:2 * NCK], ALU.add)
                P.act(sst, sst, AF.Ln, bias=EPS, scale=1.0 / 1024)
                P.act(rsd, sst, AF.Exp, scale=-0.5)
                for c in range(NCK):
                    for g in range(2):
                        yn = T_yn[g]
                        P.stt(yn, yzk[g][:, c, :], rsd[:, c:c + 1], snw_bc[:, g * 512:(g + 1) * 512], ALU.mult, ALU.mult)
                        bt = bank().bitcast(BF16)
                        for j in range(4):
                            P.transpose(bt[:, j * 128:(j + 1) * 128], yn[:, j * 128:(j + 1) * 128], identB)
                        P.copy(yT[:, g * 4:g * 4 + 4, c * 128:(c + 1) * 128],
                               bt[:, 0:512].rearrange("p (j t) -> p j t", j=4), eng=("act" if g else "dve"))
                outproj([yT[:, j, :] for j in range(8)], 1024, 8)

            if FLAGS["sgp"]:
                mixer_sg(0, 512, [([0, 1], "zero", 0), ([2, 3], "zero", 1)], 0, False)
            if FLAGS["sgs"]:
                mixer_sg(512, 1024, [(list(range(8)), "load", None)], 1, True)
            bank_mod[0] = 8

        FOFF = 22 * TOK * 2
        rowF = A.view(o_stage[0], 1024, F32)
        fnw_bc = arena(FOFF, 1024, F32)
        fin_junk = A.view(o_tmp[0], 512, F32)
        fin_small = [A.view(o_rstd[i_], 8, F32) for i_ in range(2)]
        fch = chan("fnrow")
        P.dma("sp", rowF[0:1, :], rt[112:120, :].rearrange("(o r) c -> o (r c)", o=1), fch)
        for hf in range(2):
            bk = bank()
            P.mm(bk, onesF[0:1, :], rowF[0:1, hf * 512:(hf + 1) * 512])
            P.copy(fnw_bc[:, hf * 512:(hf + 1) * 512], bk)
        norm_mod(2, SQ_OFF)
        ffn(2, f2_in, f2_out)
        for c in range(NCH):
            st = A.view(o_stage[c % 2], 1024, F32)
            sm_ = fin_small[c % 2]
            bks = []
            for half in range(2):
                bk = bank()
                bks.append(bk)
                for kk in range(4):
                    k = half * 4 + kk
                    P.transpose(bk[:, kk * 128:(kk + 1) * 128], xT[:, k, c * 128:(c + 1) * 128], identF)
                P.act(fin_junk, bk, AF.Square, accum_out=sm_[:, half:half + 1])
            P.tt(sm_[:, 2:3], sm_[:, 0:1], sm_[:, 1:2], ALU.add)
            P.act(sm_[:, 3:4], sm_[:, 2:3], AF.Ln, bias=EPS, scale=1.0 / D)
            P.act(sm_[:, 4:5], sm_[:, 3:4], AF.Exp, scale=-0.5)
            for half in range(2):
                P.stt(st[:, half * 512:(half + 1) * 512], bks[half], sm_[:, 4:5],
                      fnw_bc[:, half * 512:(half + 1) * 512], ALU.mult, ALU.mult)
            P.dma("sp", y_out[c * 128:(c + 1) * 128, :], st, stage_ch[c % 2])

        P.emit(esem)
    return nc


def make_in_maps(inputs):
    f = lambda a: np.ascontiguousarray(np.asarray(a, dtype=np.float32))
    xp = f(inputs["x_prompt"])
    xs = f(inputs["x_sample"])
    consts = host_consts()
    shared = {
        "consts": consts,
        "w_mod": f(inputs["w_mod"][0]),
        "ffn1_w_in": f(inputs["ffn1_w_in"][0]), "ffn1_w_out": f(inputs["ffn1_w_out"][0]),
        "ffn2_w_in": f(inputs["ffn2_w_in"][0]), "ffn2_w_out": f(inputs["ffn2_w_out"][0]),
        "w_in": f(inputs["w_in"][0]), "w_out": f(inputs["w_out"][0]),
        "gla_w_a2": f(inputs["gla_w_a2"][0]), "gla_b_a": f(inputs["gla_b_a"][0]),
        "gla_norm_w": f(inputs["gla_norm_w"][0]).reshape(1, 256),
        "dt_bias": f(inputs["dt_bias"][0]).reshape(1, 32),
        "a_log": f(inputs["a_log"][0]).reshape(1, 32),
        "d_skip": f(inputs["d_skip"][0]).reshape(1, 16),
        "ssd_norm_w": f(inputs["ssd_norm_w"][0]).reshape(1, 1024),
    }
    maps = []
    for i in range(8):
        rt = np.concatenate([
            f(inputs["b_mod"][0]).reshape(72, 128),
            f(inputs["c_ctx"]).reshape(8, 128),
            f(inputs["c"][i]).reshape(8, 128),
            f(inputs["norm_ffn1"][0]).reshape(8, 128),
            f(inputs["norm_mix"][0]).reshape(8, 128),
            f(inputs["norm_ffn2"][0]).reshape(8, 128),
            f(inputs["final_norm"]).reshape(8, 128),
            f(inputs["conv_w"][0]).reshape(108, 128),
            f(inputs["conv_b"][0]).reshape(12, 128),
        ], axis=0)
        m = dict(shared)
        m["xin"] = np.concatenate([xp[2 * i], xp[2 * i + 1], xs[i]], axis=0)
        m["rt"] = np.ascontiguousarray(rt)
        m["sg"] = f(inputs["state_gla"][i, 0])
        m["ss"] = f(inputs["state_ssd"][i, 0])
        maps.append(m)
    return maps


_NC_CACHE = {}


def kernel(**inputs):
    maps = make_in_maps(inputs)
    if "nc" not in _NC_CACHE:
        _NC_CACHE["nc"] = build()
    nc = _NC_CACHE["nc"]
    res = run_bass_kernel_spmd(nc, maps, core_ids=list(range(8)))
    r = res.results
    odt = np.asarray(r[0]["y"]).dtype
    y_prompt = np.zeros((16, 256, D), odt)
    y_sample = np.zeros((8, 1024, D), odt)
    nsg = np.zeros((16, 1, 2, 4, 128, 256), odt)
    nss = np.zeros((16, 1, 2, 16, 64, 128), odt)
    for i in range(8):
        y = np.asarray(r[i]["y"])
        y_prompt[2 * i] = y[0:256]
        y_prompt[2 * i + 1] = y[256:512]
        y_sample[i] = y[512:]
        nsg[2 * i:2 * i + 2, 0] = np.asarray(r[i]["nsg"])
        nss[2 * i:2 * i + 2, 0] = np.asarray(r[i]["nss"])
    return (y_prompt, y_sample, nsg, nss)
```
